# Optimizing a Trainium2 kernel written in Bass

```python
import jax, jax.numpy as jnp
from jax import lax
import numpy as np

D_MODEL = 1024
BATCH = 16
SEQ = 256
DEPTH = 4
DEC_BATCH = 8
DEC_SEQ = 2048
PAST_LEN = 512

GRID_W = 64
FFT_GROUPS = 4
FFT_GC = 128
FFT_W = FFT_GROUPS * FFT_GC
SGU_GROUPS = 4
SGU_GC = 128
SGU_W = SGU_GROUPS * SGU_GC
SGU_CHUNK = 128
GLA_H = 4
GLA_DK = 128
GLA_DV = 256
GLA_LR = 16
GLA_GATE_NORM = 16.0
GLA_CHUNK = 64
D_FF = 2816
N_BRANCH = 3
N_ADA = 6
LN_EPS = 1e-5
SPLITS = [FFT_W, SGU_W, SGU_W, GLA_H * GLA_DK, GLA_H * GLA_DK, GLA_H * GLA_DV, GLA_H * GLA_DV, 2 * GLA_LR, N_BRANCH * D_MODEL]
D_IN = sum(SPLITS)

kernel_name = "hybrid_fourier_sgu_gla_diffusion_step"


def _norm32(x):
    x32 = x.astype(jnp.float32)
    mu = jnp.mean(x32, axis=-1, keepdims=True)
    var = jnp.mean(jnp.square(x32 - mu), axis=-1, keepdims=True)
    return (x32 - mu) * lax.rsqrt(var + LN_EPS)


def _layer_norm(x, g, b):
    return (_norm32(x) * g.astype(jnp.float32) + b.astype(jnp.float32)).astype(x.dtype)


def _gla_scan(q, k, v, logg, s0):
    B, T, H, K = q.shape
    V = v.shape[-1]
    L = GLA_CHUNK
    n = T // L
    def r(a):
        return a.reshape(B, n, L, H, a.shape[-1]).transpose(1, 0, 3, 2, 4)
    q, k, v, logg = r(q), r(k), r(v), r(logg)
    G = jnp.cumsum(logg, axis=3)
    G_last = G[:, :, :, -1:, :]
    qg = q * jnp.exp(G)
    kg = k * jnp.exp(-G)
    A = jnp.einsum('nbhlk,nbhmk->nbhlm', qg, kg)
    mask = jnp.tril(jnp.ones((L, L), dtype=bool))
    A = jnp.where(mask, A, 0.0)
    intra = jnp.einsum('nbhlm,nbhmv->nbhlv', A, v)
    dS = jnp.einsum('nbhlk,nbhlv->nbhkv', k * jnp.exp(G_last - G), v)
    decay = jnp.exp(G_last[:, :, :, 0, :])

    def step(S, inp):
        d, ds = inp
        return d[..., None] * S + ds, S

    S_fin, S_start = lax.scan(step, s0, (decay, dS))
    inter = jnp.einsum('nbhlk,nbhkv->nbhlv', qg, S_start)
    o = (intra + inter).transpose(1, 0, 3, 2, 4).reshape(B, T, H, V)
    return o, S_fin


def _mixer(h, s0_f, s0_b, w_in, b_merge, w_fft_out, sgu_g, sgu_ws, sgu_b, w_sgu_out,
           gla_w2, gla_b, gla_norm_g, w_gla_out, w_o):
    B, T, D = h.shape
    idx = np.cumsum(SPLITS)[:-1].tolist()
    z = h @ w_in
    a_f, u, v, q, k, vv, r, lr, gm = jnp.split(z, idx, axis=-1)
    af = a_f.astype(jnp.float32).reshape(B, T, FFT_GROUPS, FFT_GC)
    af = jnp.fft.fft2(af, axes=(1, 3), norm="ortho").real.astype(h.dtype).reshape(B, T, FFT_W)
    y_a = af @ w_fft_out
    u = jax.nn.gelu(u)
    v = jax.nn.gelu(v)
    v = (_norm32(v) * sgu_g.astype(jnp.float32)).astype(h.dtype)
    vc = v.reshape(B, T // SGU_CHUNK, SGU_CHUNK, SGU_GROUPS, SGU_GC)
    vm = jnp.einsum('gpq,bnqgc->bnpgc', sgu_ws, vc) + sgu_b.T[:, :, None]
    y_b = (u * vm.reshape(B, T, SGU_W)) @ w_sgu_out
    qh = q.astype(jnp.float32).reshape(B, T, GLA_H, GLA_DK) * (GLA_DK ** -0.5)
    kh = k.astype(jnp.float32).reshape(B, T, GLA_H, GLA_DK)
    vh = vv.astype(jnp.float32).reshape(B, T, GLA_H, GLA_DV)
    lr2 = lr.astype(jnp.float32).reshape(B, T, 2, GLA_LR)
    zg = jnp.einsum('btdr,drk->btdk', lr2, gla_w2.astype(jnp.float32)) + gla_b.astype(jnp.float32)
    logg = (jax.nn.log_sigmoid(zg) / GLA_GATE_NORM).reshape(B, T, 2, GLA_H, GLA_DK)
    o_f, s_f = _gla_scan(qh, kh, vh, logg[:, :, 0], s0_f)
    o_b, s_b = _gla_scan(qh[:, ::-1], kh[:, ::-1], vh[:, ::-1], logg[:, ::-1, 1], s0_b)
    o = o_f + o_b[:, ::-1]
    o = o * lax.rsqrt(jnp.mean(jnp.square(o), axis=-1, keepdims=True) + LN_EPS) * gla_norm_g.astype(jnp.float32)
    o = o.reshape(B, T, GLA_H * GLA_DV).astype(h.dtype) * jax.nn.silu(r)
    y_c = o @ w_gla_out
    g = jax.nn.sigmoid(gm.astype(jnp.float32).reshape(B, T, N_BRANCH, D) + b_merge.astype(jnp.float32)).astype(h.dtype)
    m = g[:, :, 0] * y_a + g[:, :, 1] * y_b + g[:, :, 2] * y_c
    return m @ w_o, s_f, s_b


def _conv_ffn(h, w_up, w_dw, b_dw, w_down, latent):
    B, T, _ = h.shape
    a, g = jnp.split(h @ w_up, 2, axis=-1)
    if latent:
        rows = T // GRID_W
        ag = a.reshape(B, rows, GRID_W, D_FF)
        conv = lax.conv_general_dilated(ag, w_dw[:, :, None, :].astype(a.dtype), (1, 1), 'SAME',
                                        dimension_numbers=('NHWC', 'HWIO', 'NHWC'),
                                        feature_group_count=D_FF).reshape(B, T, D_FF)
    else:
        w1 = w_dw[1].astype(a.dtype)
        ap = jnp.pad(a, ((0, 0), (1, 1), (0, 0)))
        conv = ap[:, :-2] * w1[0] + ap[:, 1:-1] * w1[1] + ap[:, 2:] * w1[2]
    conv = conv + b_dw
    return (jax.nn.gelu(conv) * g) @ w_down


def _layer(x, cond, latent, s0_f, s0_b, alpha, p):
    mod = (jax.nn.silu(cond.astype(jnp.float32)) @ p['w_ada'].astype(jnp.float32) + p['b_ada'].astype(jnp.float32))
    mod = mod.reshape(cond.shape[0], 1, N_ADA, D_MODEL).astype(x.dtype)
    sh1, sc1, g1, sh2, sc2, g2 = [mod[:, :, i] for i in range(N_ADA)]
    h = (_norm32(x).astype(x.dtype) * (1.0 + sc1) + sh1)
    mix, s_f, s_b = _mixer(h, s0_f, s0_b, p['w_in'], p['b_merge'], p['w_fft_out'], p['sgu_g'], p['sgu_ws'],
                           p['sgu_b'], p['w_sgu_out'], p['gla_w2'], p['gla_b'], p['gla_norm_g'],
                           p['w_gla_out'], p['w_o'])
    x = _layer_norm(alpha * x + g1 * mix.astype(x.dtype), p['ln1_g'], p['ln1_b'])
    h2 = (_norm32(x).astype(x.dtype) * (1.0 + sc2) + sh2)
    f = _conv_ffn(h2, p['w_up'], p['w_dw'], p['b_dw'], p['w_down'], latent)
    x = _layer_norm(alpha * x + g2 * f.astype(x.dtype), p['ln2_g'], p['ln2_b'])
    return x, s_f, s_b


def setup_inputs(seed: int = 0) -> dict:
    key = jax.random.key(seed)
    ks = jax.random.split(key, 32)
    beta = (8.0 * DEPTH) ** -0.25
    f32 = jnp.float32
    nrm = lambda k, shape, s: jax.random.normal(k, shape, f32) * s
    D = D_MODEL
    return {
        "x_prompt": nrm(ks[0], (BATCH, SEQ, D), 1.0),
        "x_sample": nrm(ks[1], (DEC_BATCH, DEC_SEQ, D), 1.0),
        "state_gla": nrm(ks[2], (DEC_BATCH, DEPTH, 2, GLA_H, GLA_DK, GLA_DV), 0.5),
        "c": nrm(ks[3], (DEC_BATCH, D), 1.0),
        "c_ctx": nrm(ks[4], (D,), 1.0),
        "w_in": nrm(ks[5], (DEPTH, D, D_IN), D ** -0.5),
        "b_merge": nrm(ks[6], (DEPTH, N_BRANCH, D), 0.02),
        "w_fft_out": nrm(ks[7], (DEPTH, FFT_W, D), FFT_W ** -0.5),
        "sgu_g": 1.0 + nrm(ks[8], (DEPTH, SGU_W), 0.02),
        "sgu_ws": nrm(ks[9], (DEPTH, SGU_GROUPS, SGU_CHUNK, SGU_CHUNK), SGU_CHUNK ** -0.5),
        "sgu_b": 1.0 + nrm(ks[10], (DEPTH, SGU_GROUPS, SGU_CHUNK), 0.02),
        "w_sgu_out": nrm(ks[11], (DEPTH, SGU_W, D), SGU_W ** -0.5),
        "gla_w2": nrm(ks[12], (DEPTH, 2, GLA_LR, GLA_H * GLA_DK), GLA_LR ** -0.5),
        "gla_b": nrm(ks[13], (DEPTH, 2, GLA_H * GLA_DK), 0.02),
        "gla_norm_g": 1.0 + nrm(ks[14], (DEPTH, GLA_DV), 0.02),
        "w_gla_out": nrm(ks[15], (DEPTH, GLA_H * GLA_DV, D), (GLA_H * GLA_DV) ** -0.5),
        "w_o": nrm(ks[16], (DEPTH, D, D), beta * D ** -0.5),
        "ln1_g": 1.0 + nrm(ks[17], (DEPTH, D), 0.02),
        "ln1_b": nrm(ks[18], (DEPTH, D), 0.02),
        "w_ada": nrm(ks[19], (DEPTH, D, N_ADA * D), 0.5 * D ** -0.5),
        "b_ada": nrm(ks[20], (DEPTH, N_ADA * D), 0.02),
        "w_up": nrm(ks[21], (DEPTH, D, 2 * D_FF), D ** -0.5),
        "w_dw": nrm(ks[22], (DEPTH, 3, 3, D_FF), 1.0 / 3.0),
        "b_dw": nrm(ks[23], (DEPTH, D_FF), 0.02),
        "w_down": nrm(ks[24], (DEPTH, D_FF, D), beta * D_FF ** -0.5),
        "ln2_g": 1.0 + nrm(ks[25], (DEPTH, D), 0.02),
        "ln2_b": nrm(ks[26], (DEPTH, D), 0.02),
    }


def reference(x_prompt, x_sample, state_gla, c, c_ctx, w_in, b_merge, w_fft_out, sgu_g, sgu_ws, sgu_b,
              w_sgu_out, gla_w2, gla_b, gla_norm_g, w_gla_out, w_o, ln1_g, ln1_b, w_ada, b_ada,
              w_up, w_dw, b_dw, w_down, ln2_g, ln2_b):
    alpha = (2.0 * DEPTH) ** 0.25
    xp = x_prompt
    xs = x_sample
    zeros = jnp.zeros((x_prompt.shape[0], GLA_H, GLA_DK, GLA_DV), jnp.float32)
    cond_ctx = c_ctx[None, :]
    new_states = []
    for l in range(DEPTH):
        p = {'w_in': w_in[l], 'b_merge': b_merge[l], 'w_fft_out': w_fft_out[l], 'sgu_g': sgu_g[l],
             'sgu_ws': sgu_ws[l], 'sgu_b': sgu_b[l], 'w_sgu_out': w_sgu_out[l], 'gla_w2': gla_w2[l],
             'gla_b': gla_b[l], 'gla_norm_g': gla_norm_g[l], 'w_gla_out': w_gla_out[l], 'w_o': w_o[l],
             'ln1_g': ln1_g[l], 'ln1_b': ln1_b[l], 'w_ada': w_ada[l], 'b_ada': b_ada[l], 'w_up': w_up[l],
             'w_dw': w_dw[l], 'b_dw': b_dw[l], 'w_down': w_down[l], 'ln2_g': ln2_g[l], 'ln2_b': ln2_b[l]}
        xp, s_f, s_b = _layer(xp, cond_ctx, False, zeros, zeros, alpha, p)
        new_states.append(jnp.stack([s_f, s_b], axis=1))
        xs, _, _ = _layer(xs, c, True, state_gla[:, l, 0].astype(jnp.float32),
                          state_gla[:, l, 1].astype(jnp.float32), alpha, p)
    new_state_gla = jnp.stack(new_states, axis=1).astype(x_prompt.dtype)
    y_prompt = xp.astype(x_prompt.dtype)
    y_sample = xs.astype(x_sample.dtype)
    return (y_prompt, y_sample, new_state_gla)
```

```python
import contextlib
import math
import numpy as np
import ml_dtypes
import concourse.bass as bass
import concourse.mybir as mybir
from concourse.bass_utils import run_bass_kernel_spmd

F32 = mybir.dt.float32
BF16 = mybir.dt.bfloat16
AF = mybir.ActivationFunctionType
ALU = mybir.AluOpType

D = 1024
DEPTH = 4
NTOK = 2560
NTT = 20
NJ = 5
D_FF = 2816
NC_FF = 22
LN_EPS = 1e-5
ALPHA = (2.0 * DEPTH) ** 0.25
SEQS = [(0, 2048, True), (2048, 256, False), (2304, 256, False)]
PL = 48 + 24 + 32 + 22 + 198
P_BADA, P_BM, P_LN, P_BDW, P_WDW = 0, 48, 72, 104, 126
B_AF, B_U, B_V, B_Q, B_K, B_VV, B_R, B_LR, B_GM = 0, 4, 8, 12, 16, 20, 28, 36, 37

COMPUTE = ("pe", "act", "dve", "pool")
NDMASEM = 6


class _I:
    __slots__ = ("eng", "fn", "deps", "signal", "sigval", "is_dma", "dsem")

    def __init__(self, eng, fn, is_dma):
        self.eng = eng
        self.fn = fn
        self.deps = []
        self.signal = False
        self.sigval = 0
        self.is_dma = is_dma
        self.dsem = None


class Sched:
    def __init__(self, nc):
        self.nc = nc
        self.q = {e: [] for e in ("pe", "act", "dve", "pool", "sp")}
        self.lastw = {}
        self.readers = {}
        self.arena_names = {"M", "ps"}
        self.cur_fence = None
        self.n = 0

    def _name(self, k):
        return k[0] if isinstance(k, tuple) else k

    def add(self, eng, fn, reads=(), writes=(), dma=False):
        ins = _I(eng, fn, dma)
        deps = set()
        for k in list(reads) + list(writes):
            if k not in self.lastw and self.cur_fence is not None and self._name(k) in self.arena_names:
                self.lastw[k] = self.cur_fence
        for r in reads:
            w = self.lastw.get(r)
            if w is not None:
                deps.add(w)
        for w_ in writes:
            w = self.lastw.get(w_)
            if w is not None:
                deps.add(w)
            for rd in self.readers.get(w_, ()):
                deps.add(rd)
        deps.discard(ins)
        for d in deps:
            if d.eng == "pe" and eng == "pe" and not d.is_dma and not dma:
                continue
            d.signal = True
            ins.deps.append(d)
        for r in reads:
            self.readers.setdefault(r, []).append(ins)
        for w_ in writes:
            self.lastw[w_] = ins
            self.readers[w_] = []
        self.q[eng].append(ins)
        self.n += 1
        return ins

    def dma(self, out, in_, reads=(), writes=(), q="sp"):
        return self.add(q, lambda e: e.dma_start(out=out, in_=in_), reads, writes, dma=True)

    def fence(self, dummy_ap):
        keys = [k for k in set(self.lastw) | set(self.readers) if self._name(k) in self.arena_names]
        f = self.add("dve", lambda e: e.memset(dummy_ap, 0.0), reads=[], writes=keys + ["__fence__"])
        for k in keys:
            self.lastw.pop(k, None)
            self.readers.pop(k, None)
        self.cur_fence = f

    def emit(self):
        nc = self.nc
        with contextlib.ExitStack() as es:
            csem = {e: es.enter_context(nc.semaphore("c_" + e)) for e in COMPUTE}
            dsems = {}
            dma_lists = {}
            for qn in self.q:
                lst = [i for i in self.q[qn] if i.is_dma]
                if lst:
                    dsems[qn] = [es.enter_context(nc.semaphore("d_%s_%d" % (qn, k))) for k in range(NDMASEM)]
                    dma_lists[qn] = lst
                    for k, i in enumerate(lst):
                        i.dsem = (dsems[qn][k % NDMASEM], 16 * (k // NDMASEM + 1))
            for e in COMPUTE:
                cnt = 0
                for i in self.q[e]:
                    if not i.is_dma and i.signal:
                        cnt += 1
                        i.sigval = cnt
            block = es.enter_context(nc.Block())

            def run_engine(qn, eng):
                waited = {}
                dcount = 0

                def need(sem, val):
                    key = id(sem)
                    if waited.get(key, 0) >= val:
                        return
                    waited[key] = val
                    eng.wait_ge(sem, val)

                for i in self.q[qn]:
                    if i.is_dma:
                        k = dcount
                        dcount += 1
                        if k >= NDMASEM:
                            prev = dma_lists[qn][k - NDMASEM]
                            need(prev.dsem[0], prev.dsem[1])
                    for d in i.deps:
                        if d.is_dma:
                            need(d.dsem[0], d.dsem[1])
                        else:
                            need(csem[d.eng], d.sigval)
                    h = i.fn(eng)
                    if i.is_dma:
                        h.then_inc(i.dsem[0], 16)
                    elif i.signal:
                        h.then_inc(csem[i.eng], 1)
                if qn in dma_lists:
                    for i in dma_lists[qn][-NDMASEM:]:
                        need(i.dsem[0], i.dsem[1])

            @block.tensor
            def _(e):
                run_engine("pe", e)

            @block.scalar
            def _(e):
                run_engine("act", e)

            @block.vector
            def _(e):
                run_engine("dve", e)

            @block.gpsimd
            def _(e):
                run_engine("pool", e)

            @block.sync
            def _(e):
                run_engine("sp", e)


def build(n_layers=DEPTH, taps=(), stop_after=None):
    nc = bass.Bass("TRN2", target_bir_lowering=False)
    S = Sched(nc)
    L = DEPTH

    def din(name, shape, dtype=F32):
        return nc.dram_tensor(name, list(shape), dtype, kind="ExternalInput").ap()

    def dout(name, shape, dtype=F32):
        return nc.dram_tensor(name, list(shape), dtype, kind="ExternalOutput").ap()

    def dscr(name, shape, dtype):
        return nc.dram_tensor(name, list(shape), dtype).ap()

    xin = din("xin", [NJ, 128, 8, 512])
    cT_d = din("cT", [128, 8, 2])
    st0 = din("st0", [L, 2, 4, 128, 256])
    WIN = din("WIN", [L, 61, 128, 1024])
    WFFT = din("WFFT", [L, 8, 128, 512])
    WSGU = din("WSGU", [L, 8, 128, 512])
    WGLA = din("WGLA", [L, 8, 128, 1024])
    WO = din("WO", [L, 8, 128, 1024])
    WUP = din("WUP", [L, 44, 128, 1024])
    WDOWN = din("WDOWN", [L, 8, 128, 2816])
    WADA = din("WADA", [L, 48, 128, 1024])
    PAR_d = din("PAR", [128, L * PL])
    SGUG_d = din("SGUG", [L, 128, 512])
    GNG_d = din("GNG", [L, 128, 256])
    WST_d = din("WST", [L, 128, 4, 128])
    W2P_d = din("W2P", [L, 32, 2, 512])
    GLB_d = din("GLB", [L, 1, 1024])
    SGUB_d = din("SGUB", [L, 1, 512])
    IDB_d = din("IDB", [128, 128], BF16)
    CSC_d = din("CSC", [128, 256], BF16)
    DFTP_d = din("DFTP", [128, 2, 2, 256], BF16)
    TMS_d = din("TMS", [128, 2, 2, 128], BF16)
    MASKA_d = din("MASKA", [128, 2, 4, 128], BF16)
    DFTL_d = din("DFTL", [8, 2, 128, 16, 256], BF16)
    yout = dout("yout", [NJ, 128, 8, 512])
    nsout = dout("nsout", [2, L, 2, 4, 128, 256])
    XD = dscr("XD", [NJ, 128, 8, 512], F32)
    KVR = dscr("KVR", [NTT, 128, 2560], BF16)
    SPD = dscr("SPD", [NTT, 128, 2, 512], BF16)
    OD = dscr("OD", [2, NTT, 128, 1024], F32)
    PD = dscr("PD", [NC_FF, 128, NTOK], BF16)
    tap_out = {}

    cur = [16512]
    LIMIT = 229344

    def salloc(name, shape, dtype):
        nb = int(np.prod(shape[1:])) * (4 if dtype == F32 else 2)
        nb = (nb + 31) // 32 * 32
        off = cur[0]
        cur[0] += nb
        assert cur[0] <= LIMIT, (name, cur[0])
        return nc.alloc_sbuf_tensor_at(name, list(shape), dtype, offset=off)

    IDB = salloc("IDB", [128, 128], BF16)
    ONESB = salloc("ONESB", [128, 128], BF16)
    CSC = salloc("CSC", [128, 256], BF16)
    DFTP = salloc("DFTP", [128, 2, 2, 256], BF16)
    TMS = salloc("TMS", [128, 2, 2, 128], BF16)
    MASKA = salloc("MASKA", [128, 2, 4, 128], BF16)
    PAR = salloc("PAR", [128, L * PL], F32)
    MOD = salloc("MOD", [128, L, 48, 2], F32)
    CT = salloc("CT", [128, 8, 2], F32)
    SCT = salloc("SCT", [128, 8, 2], F32)
    SCB = salloc("SCB", [128, 8, 2], BF16)
    DUMMY = salloc("DUMMY", [128, 8], F32)
    NEGH = salloc("NEGH", [128, 8], F32)
    SGUG = salloc("SGUG", [128, 512], F32)
    GNG = salloc("GNG", [128, 256], F32)
    WST = salloc("WST", [128, 4, 128], BF16)
    W2P = salloc("W2P", [32, 2, 512], BF16)
    GLB = salloc("GLB", [1, 1024], BF16)
    SGUB = salloc("SGUB", [1, 512], BF16)
    H = salloc("H", [128, 8, NTOK], BF16)
    M = salloc("M", [128, 8, NTOK], BF16)
    M_OFF = cur[0] - 8 * NTOK * 2
    RING = salloc("RING", [128, 8, 1024], BF16)
    ARENA0 = cur[0]
    acur = [ARENA0]
    seqno = [0]

    def aalloc(name, shape, dtype):
        nb = int(np.prod(shape[1:])) * (4 if dtype == F32 else 2)
        nb = (nb + 31) // 32 * 32
        off = acur[0]
        acur[0] += nb
        assert acur[0] <= LIMIT, (name, acur[0] - LIMIT)
        S.arena_names.add(name)
        seqno[0] += 1
        return nc.alloc_sbuf_tensor_at("%s_%d" % (name, seqno[0]), list(shape), dtype, offset=off)

    def new_phase():
        S.fence(DUMMY[:, 0:1])
        acur[0] = ARENA0

    ps = [nc.alloc_psum_tensor("ps%d" % i, [128, 512], F32) for i in range(8)]
    PSK = [("ps", i) for i in range(8)]

    def mm(out, lhsT, rhs, st, sp, r, w):
        S.add("pe", lambda e: e.matmul(out, lhsT, rhs, start=st, stop=sp), r, w)

    def act(out, in_, func, r, w, **kw):
        S.add("act", lambda e: e.activation(out=out, in_=in_, func=func, **kw), r, w)

    def tt(out, in0, in1, op, r, w, eng="dve"):
        S.add(eng, lambda e: e.tensor_tensor(out=out, in0=in0, in1=in1, op=op), r, w)

    def ts(out, in0, s1, s2, op0, op1, r, w, eng="dve"):
        S.add(eng, lambda e: e.tensor_scalar(out=out, in0=in0, scalar1=s1, scalar2=s2, op0=op0, op1=op1), r, w)

    def stt(out, in0, sc, in1, op0, op1, r, w):
        S.add("dve", lambda e: e.scalar_tensor_tensor(out=out, in0=in0, scalar=sc, in1=in1, op0=op0, op1=op1), r, w)

    def cp(out, in_, r, w, eng="dve"):
        S.add(eng, lambda e: e.tensor_copy(out=out, in_=in_), r, w)

    def tap(name, ap, shape, dtype, reads):
        if name in taps:
            t = dout("tap_" + name, shape, dtype)
            tap_out[name] = t
            if len(shape) == 3 and shape[1] <= 8:
                for i_ in range(min(shape[1], int(os.environ.get("K_TAPN", "8")))):
                    S.dma(t[:, i_], ap[:, i_], reads=reads, writes=[("tap", name, i_)])
            else:
                S.dma(t, ap, reads=reads, writes=[("tap", name)])

    rptr = [0]

    def ring_load(src, nslots, align=1, nelem=1024):
        p = (rptr[0] + align - 1) // align * align
        if p + nslots > 8:
            p = 0
        rptr[0] = (p + nslots) % 8
        keys = [("ring", s) for s in range(p, p + nslots)]
        return p, keys

    def wload1(src_ap, nelem=1024):
        s, keys = ring_load(src_ap, 1)
        S.dma(RING[:, s, 0:nelem], src_ap, writes=keys, q="pool")
        return s, keys

    def stream(items, loader, compute, depth):
        hs = {}
        n = len(items)
        for i in range(min(depth, n)):
            hs[i] = loader(items[i])
        for i in range(n):
            compute(items[i], hs.pop(i))
            if i + depth < n:
                hs[i + depth] = loader(items[i + depth])

    def grp_of(j):
        return 0 if j < 4 else 1

    def modap(l, slot, c, g):
        return MOD[:, l, slot * 8 + c, g:g + 1]

    def parap(l, off, idx):
        return PAR[:, l * PL + off + idx:l * PL + off + idx + 1]

    S.dma(IDB[:], IDB_d, writes=["IDB"])
    S.dma(CSC[:], CSC_d, writes=["CSC"])
    S.dma(DFTP[:], DFTP_d, writes=["DFTP"])
    S.dma(TMS[:], TMS_d, writes=["TMS"])
    S.dma(MASKA[:], MASKA_d, writes=["MASKA"])
    S.dma(PAR[:], PAR_d, writes=["PAR"])
    S.dma(CT[:], cT_d, writes=["CT"])
    S.add("dve", lambda e: e.memset(ONESB[:], 1.0), writes=["ONESB"])
    S.add("dve", lambda e: e.memset(NEGH[:], -0.5), writes=["NEGH"])
    act(SCT[:], CT[:], AF.Silu, ["CT"], ["SCT"])
    cp(SCB[:], SCT[:], ["SCT"], ["SCB"])

    def mod_block(l, j, h):
        s_, keys = h
        for kc in range(8):
            mm(ps[0][:, 2 * j:2 * j + 2], RING[:, s_, kc * 128:(kc + 1) * 128], SCB[:, kc, :], kc == 0, kc == 7,
               keys + ["SCB"], [PSK[0]])

    def mod_finish(l):
        pv = ps[0][:, 0:96].rearrange("p (j g) -> p j g", g=2)
        for g in range(2):
            tt(MOD[:, l, :, g], pv[:, :, g], PAR[:, l * PL + P_BADA:l * PL + P_BADA + 48], ALU.add,
               [PSK[0], "PAR"], [("MOD", l)])
        for slot in (1, 4):
            ts(MOD[:, l, slot * 8:slot * 8 + 8, :], MOD[:, l, slot * 8:slot * 8 + 8, :], 1.0, 0.0, ALU.add, ALU.add,
               [("MOD", l)], [("MOD", l)])
        for slot in (2, 5):
            ts(MOD[:, l, slot * 8:slot * 8 + 8, :], MOD[:, l, slot * 8:slot * 8 + 8, :], 1.0 / ALPHA, 0.0, ALU.mult, ALU.add,
               [("MOD", l)], [("MOD", l)])

    def phase_mod():
        new_phase()
        stream(list(range(48)), lambda j: wload1(WADA[0, j]), lambda j, h: mod_block(0, j, h), 6)
        mod_finish(0)

    def xk(b):
        return [("XT", b, c) for c in range(8)]

    def norm_tiles(nxt=2, ext=None):
        t = {}
        t["XT"] = [aalloc("XT", [128, 8, 512], F32) for _ in range(nxt)]
        if ext is None:
            t["XB"] = aalloc("XB", [128, 8, 512], BF16)
            t["SQ"] = aalloc("SQ", [128, 8, 512], BF16)
        else:
            S.arena_names.add("XB")
            S.arena_names.add("SQ")
            t["XB"], t["SQ"] = ext
        t["MEAN"] = aalloc("MEAN", [128, 512], F32)
        t["MSQ"] = aalloc("MSQ", [128, 512], F32)
        t["RSTD"] = aalloc("RSTD", [128, 512], F32)
        t["T"] = [aalloc("T", [128, 512], F32) for _ in range(2)]
        t["tk"] = 0
        return t

    import os
    DBG = int(os.environ.get("K_DBG", "99"))

    def stats(t, b, pa, pb, eps=LN_EPS):
        stats_a(t, b)
        stats_b(t, b, pa, pb, eps)

    def stats_a(t, b):
        XT = t["XT"][b]
        cp(t["XB"][:], XT[:], xk(b), ["XB"])
        act(t["SQ"][:], XT[:], AF.Square, xk(b), ["SQ"])

    def stats_b(t, b, pa, pb, eps=LN_EPS):
        XT = t["XT"][b]
        for kc in range(8):
            mm(ps[pa][:], ONESB[:], t["XB"][:, kc, :], kc == 0, kc == 7, ["ONESB", "XB"], [PSK[pa]])
        for kc in range(8):
            mm(ps[pb][:], ONESB[:], t["SQ"][:, kc, :], kc == 0, kc == 7, ["ONESB", "SQ"], [PSK[pb]])
        ts(t["MEAN"][:], ps[pa][:], 1.0 / D, 0.0, ALU.mult, ALU.add, [PSK[pa]], ["MEAN"])
        if DBG <= 2:
            return
        tt(t["MSQ"][:], t["MEAN"][:], t["MEAN"][:], ALU.mult, ["MEAN"], ["MSQ"])
        stt(t["RSTD"][:], ps[pb][:], 1.0 / D, t["MSQ"][:], ALU.mult, ALU.subtract, [PSK[pb], "MSQ"], ["RSTD"])
        act(t["RSTD"][:], t["RSTD"][:], AF.Sqrt, ["RSTD"], ["RSTD"], bias=eps)
        S.add("dve", lambda e: e.reciprocal(out=ps[pb][:], in_=t["RSTD"][:]), ["RSTD"], [PSK[pb]])
        stt(ps[pa][:], t["MEAN"][:], -1.0, ps[pb][:], ALU.mult, ALU.mult, ["MEAN", PSK[pb]], [PSK[pa]])
        t["pab"] = (pa, pb)

    def apply_norm(t, b, outs, okeys, scales, biases, extra_reads):
        XT = t["XT"][b]
        if DBG <= 3:
            return
        for c in range(8):
            k = t["tk"] % 2
            t["tk"] += 1
            T = t["T"][k]
            pa, pb = t["pab"]
            tt(T[:], XT[:, c, :], ps[pb][:], ALU.mult, [("XT", b, c), PSK[pb]], [("T", k)])
            tt(T[:], T[:], ps[pa][:], ALU.add, [("T", k), PSK[pa]], [("T", k)])
            if DBG <= 4:
                continue
            act(outs[c], T[:], AF.Identity, [("T", k)] + extra_reads, [okeys[c]], scale=scales[c], bias=biases[c])

    def hkeys(j):
        return [("H", c, j) for c in range(8)]

    def mod_h(t, b, j, l, s_scale, s_shift):
        g = grp_of(j)
        stats(t, b, 0, 1)
        apply_norm(t, b, [H[:, c, j * 512:(j + 1) * 512] for c in range(8)], hkeys(j),
                   [modap(l, s_scale, c, g) for c in range(8)], [modap(l, s_shift, c, g) for c in range(8)],
                   [("MOD", l)])

    def phase_L0():
        new_phase()
        t = norm_tiles()
        for j in range(NJ):
            b = j % 2
            S.dma(t["XT"][b][:], xin[j], writes=xk(b))
            mod_h(t, b, j, 0, 1, 0)

    def phase_R(l, which):
        new_phase()
        last = (which == 2 and l == n_layers - 1)
        if which == 2:
            S.arena_names.add("PT")
            ext = (nc.alloc_sbuf_tensor_at("XBm%d" % l, [128, 8, 512], BF16, offset=M_OFF + 22528),
                   nc.alloc_sbuf_tensor_at("SQm%d" % l, [128, 8, 512], BF16, offset=M_OFF + 22528 + 8192))
            t = norm_tiles(3, ext)
            PT = [nc.alloc_sbuf_tensor_at("PT%d_%d" % (l, which), [128, NC_FF, 512], BF16, offset=M_OFF),
                  aalloc("PT", [128, NC_FF, 512], BF16)]
        else:
            t = norm_tiles(3)
        gslot = 2 if which == 1 else 5
        lnoff = 0 if which == 1 else 16
        ptk = [0]
        ptsel = {}

        def load_tile(j):
            b = j % 3
            XT = t["XT"][b]
            src = xin if (l == 0 and which == 1) else XD
            S.dma(XT[:], src[j], reads=[("XD", j)], writes=xk(b))
            if which == 2:
                i = j % 2
                for q4 in range(2):
                    S.dma(PT[i][:, q4 * 11:(q4 + 1) * 11, :],
                          PD[q4 * 11:(q4 + 1) * 11, :, j * 512:(j + 1) * 512].rearrange("c p n -> p c n"),
                          reads=[("PD", c) for c in range(NC_FF)], writes=[("PT", i, q4)])

        def ln_step(j, step):
            b = j % 3
            g = grp_of(j)
            XT = t["XT"][b]
            if step == 0:
                stats_a(t, b)
            elif step == 1:
                stats_b(t, b, 0, 1, LN_EPS / (ALPHA * ALPHA))
                apply_norm(t, b, [XT[:, c, :] for c in range(8)], xk(b),
                           [parap(l, P_LN, lnoff + c) for c in range(8)], [parap(l, P_LN, lnoff + 8 + c) for c in range(8)],
                           ["PAR"])
                if last:
                    S.dma(yout[j], XT[:], reads=xk(b), writes=[("yout", j)])
                else:
                    S.dma(XD[j], XT[:], reads=xk(b), writes=[("XD", j)])
                    stats_a(t, b)
                if which == 1 and j == 0:
                    tap("x1_%d" % l, XT[:], [128, 8, 512], F32, xk(b))
            elif step == 2 and not last:
                stats_b(t, b, 0, 1)
                if which == 1:
                    sc, sh, ll = 4, 3, l
                else:
                    sc, sh, ll = 1, 0, l + 1
                apply_norm(t, b, [H[:, c, j * 512:(j + 1) * 512] for c in range(8)], hkeys(j),
                           [modap(ll, sc, c, g) for c in range(8)], [modap(ll, sh, c, g) for c in range(8)], [("MOD", ll)])

        RINGF = RING[:].rearrange("p a b -> p (a b)")
        HW = 11 * 128
        hcnt = [0]
        allring = [("ring", s_) for s_ in range(8)]

        def load_half(j, c, hf):
            hb = hcnt[0] % 5
            first = hcnt[0] < 5
            hcnt[0] += 1
            S.dma(RINGF[:, hb * HW:(hb + 1) * HW], WDOWN[l, c][:, hf * HW:(hf + 1) * HW],
                  writes=[("ringh", hb)] + (allring if first else []), q="pool")
            return hb

        def loader(item):
            j, c = item
            if which == 1:
                return wload1(WO[l, c])
            return [load_half(j, c, 0), load_half(j, c, 1)]

        def compute(item, h):
            j, c = item
            b = j % 3
            g = grp_of(j)
            XT = t["XT"][b]
            if which == 1:
                s_, keys = h
            if c == 0:
                if j + 1 < NJ:
                    pass
                if j > 0:
                    ln_step(j - 1, 0)
            pk = 2 + c % 4
            if which == 1:
                for kc in range(8):
                    mm(ps[pk][:], RING[:, s_, kc * 128:(kc + 1) * 128], M[:, kc, j * 512:(j + 1) * 512], kc == 0, kc == 7,
                       keys + [("M", kc, j)], [PSK[pk]])
            else:
                for kc in range(NC_FF):
                    hb = h[kc // 11]
                    o_ = hb * HW + (kc % 11) * 128
                    mm(ps[pk][:], RINGF[:, o_:o_ + 128], PT[j % 2][:, kc, :],
                       kc == 0, kc == NC_FF - 1, [("ringh", hb), ("PT", j % 2, kc // 11)], [PSK[pk]])
            stt(XT[:, c, :], ps[pk][:], modap(l, gslot, c, g), XT[:, c, :], ALU.mult, ALU.add,
                [PSK[pk], ("XT", b, c), ("MOD", l)], [("XT", b, c)])
            if j > 0 and c == 1:
                ln_step(j - 1, 1)
            if j > 0 and c == 6:
                ln_step(j - 1, 2)
            if c == 7 and j + 1 < NJ:
                pass

        load_tile(0)
        items = []
        for j in range(NJ):
            for c in range(8):
                items.append((j, c))
        hs = {}
        depth = 4 if which == 1 else 2
        n = len(items)
        for i in range(min(depth, n)):
            hs[i] = loader(items[i])
        for i in range(n):
            j, c = items[i]
            if c == 0 and j + 1 < NJ:
                load_tile(j + 1)
            compute(items[i], hs.pop(i))
            if i + depth < n:
                hs[i + depth] = loader(items[i + depth])
        for step in range(3):
            ln_step(NJ - 1, step)
        if which == 2:
            S.add("dve", lambda e: e.memset(DUMMY[:, 1:2], 0.0), [], [("ringh", hb_) for hb_ in range(5)] + allring)

    def merge_phase(l, br, wsrc, nk, rhs_of, rhs_keys, first):
        G = [aalloc("G", [128, 512], BF16) for _ in range(4)]
        TMP = [aalloc("TMPM", [128, 512], BF16) for _ in range(2)]
        cnt = [0]

        def loader(c):
            a = wload1(wsrc[l, c], nelem=nk * 128)
            g_ = wload1(WIN[l, B_GM + br * 8 + c])
            return a, g_

        def compute(c, h):
            (sy, ky), (sg, kg) = h
            for j in range(NJ):
                i = cnt[0] % 4
                cnt[0] += 1
                py, pg = (cnt[0] % 4) * 2, (cnt[0] % 4) * 2 + 1
                for k in range(nk):
                    mm(ps[py][:], RING[:, sy, k * 128:(k + 1) * 128], rhs_of(k, j), k == 0, k == nk - 1,
                       ky + rhs_keys(k, j), [PSK[py]])
                for kc in range(8):
                    mm(ps[pg][:], RING[:, sg, kc * 128:(kc + 1) * 128], H[:, kc, j * 512:(j + 1) * 512], kc == 0, kc == 7,
                       kg + [("H", kc, j)], [PSK[pg]])
                act(G[i][:], ps[pg][:], AF.Sigmoid, [PSK[pg], "PAR"], [("G", i)], bias=parap(l, P_BM, br * 8 + c))
                Mv = M[:, c, j * 512:(j + 1) * 512]
                if first:
                    tt(Mv, ps[py][:], G[i][:], ALU.mult, [PSK[py], ("G", i)], [("M", c, j)])
                else:
                    tt(TMP[i % 2][:], ps[py][:], G[i][:], ALU.mult, [PSK[py], ("G", i)], [("TMPM", i % 2)])
                    tt(Mv, Mv, TMP[i % 2][:], ALU.add, [("M", c, j), ("TMPM", i % 2)], [("M", c, j)])
        stream(list(range(8)), loader, compute, 2)

    def phase_A(l):
        new_phase()
        T1 = aalloc("T1", [128, 4, NTOK], BF16)
        ACS = aalloc("ACS", [128, 16, 4, 256], BF16)
        DFB = [aalloc("DFB", [128, 16, 256], BF16) for _ in range(3)]
        ec = [0]

        def compute0(gi, h):
            s, keys = h
            for j in range(NJ):
                pk = 2 + ec[0] % 4
                for kc in range(8):
                    mm(ps[pk][:], RING[:, s, kc * 128:(kc + 1) * 128], H[:, kc, j * 512:(j + 1) * 512], kc == 0, kc == 7,
                       keys + [("H", kc, j)], [PSK[pk]])
                if ec[0] % 2 == 0:
                    act(T1[:, gi, j * 512:(j + 1) * 512], ps[pk][:], AF.Copy, [PSK[pk]], [("T1", gi, j)])
                else:
                    cp(T1[:, gi, j * 512:(j + 1) * 512], ps[pk][:], [PSK[pk]], [("T1", gi, j)])
                ec[0] += 1
        stream(list(range(4)), lambda gi: wload1(WIN[l, B_AF + gi]), compute0, 4)
        dk = [0]
        for (off, T, latent) in SEQS:
            ntc = T // 128
            for tc in range(ntc):
                tg = (off + tc * 128)
                j = tg // 512
                for half in range(2):
                    pk = 2 + (2 * tc + half) % 4
                    for gg in range(2):
                        gi = half * 2 + gg
                        mm(ps[pk][:, gg * 256:(gg + 1) * 256], T1[:, gi, tg:tg + 128], CSC[:], True, True,
                           [("T1", gi, j), "CSC"], [PSK[pk]])
                    dst = ACS[:, tc, half * 2:half * 2 + 2, :]
                    src = ps[pk][:].rearrange("p (g n) -> p g n", g=2)
                    if (tc + half) % 2 == 0:
                        act(dst, src, AF.Copy, [PSK[pk]], [("ACS", tc, half)])
                    else:
                        cp(dst, src, [PSK[pk]], [("ACS", tc, half)])
            scale = 1.0 / math.sqrt(T * 128.0)
            ntile = T // 256
            for i in range(ntile):
                if latent:
                    bufs = []
                    for cs in range(2):
                        bi = dk[0] % 3
                        dk[0] += 1
                        S.dma(DFB[bi][:], DFTL_d[i, cs], writes=[("DFB", bi)])
                        bufs.append((DFB[bi], [("DFB", bi)]))
                else:
                    bufs = [(DFTP[:, cs], ["DFTP"]) for cs in range(2)]
                tok0 = off + i * 256
                j = tok0 // 512
                for cs in range(2):
                    dbuf, dkeys = bufs[cs]
                    for gi in range(4):
                        pk = 4 + gi
                        for tc in range(ntc):
                            mm(ps[pk][:, 0:256], ACS[:, tc, gi, cs * 128:(cs + 1) * 128], dbuf[:, tc, :],
                               cs == 0 and tc == 0, cs == 1 and tc == ntc - 1,
                               [("ACS", tc, gi // 2)] + dkeys, [PSK[pk]])
                for gi in range(4):
                    pk = 4 + gi
                    src = ps[pk][:, 0:256]
                    act(T1[:, gi, tok0:tok0 + 256], src, AF.Copy, [PSK[pk]], [("T1", gi, j)], scale=scale)
        tap("afo_%d" % l, T1[:], [128, 4, NTOK], BF16, [("T1", gi, j) for gi in range(4) for j in range(NJ)])
        merge_phase(l, 0, WFFT, 4, lambda k, j: T1[:, k, j * 512:(j + 1) * 512], lambda k, j: [("T1", k, j)], True)

    def phase_B(l):
        new_phase()
        T1 = aalloc("T1", [128, 4, NTOK], BF16)
        VG = [aalloc("VG", [128, 512], F32) for _ in range(2)]
        VN = [aalloc("VN", [128, 512], BF16) for _ in range(2)]
        BST = [aalloc("BST", [128, 8], F32) for _ in range(2)]
        S.dma(SGUG[:], SGUG_d[l], writes=["SGUG"])
        S.dma(WST[:], WST_d[l], writes=["WST"], q="pool")
        S.dma(SGUB[:], SGUB_d[l], writes=["SGUB"], q="pool")
        ec = [0]

        def compute_u(gi, h):
            s, keys = h
            for j in range(NJ):
                pk = 2 + ec[0] % 4
                ec[0] += 1
                for kc in range(8):
                    mm(ps[pk][:], RING[:, s, kc * 128:(kc + 1) * 128], H[:, kc, j * 512:(j + 1) * 512], kc == 0, kc == 7,
                       keys + [("H", kc, j)], [PSK[pk]])
                act(T1[:, gi, j * 512:(j + 1) * 512], ps[pk][:], AF.Gelu_apprx_tanh, [PSK[pk]], [("T1", gi, j)])
        stream(list(range(4)), lambda gi: wload1(WIN[l, B_U + gi]), compute_u, 4)
        s0, vkeys = ring_load(None, 4, align=4)
        for gi in range(4):
            S.dma(RING[:, s0 + gi, :], WIN[l, B_V + gi], writes=[("ring", s0 + gi)], q="pool")
        def v_mm(tt_):
            j = tt_ // 4
            i = tt_ % 2
            pk = 2 + tt_ % 2
            for kc in range(8):
                mm(ps[pk][:], H[:, kc, tt_ * 128:(tt_ + 1) * 128], RING[:, s0:s0 + 4, kc * 128:(kc + 1) * 128], kc == 0, kc == 7,
                   vkeys + [("H", kc, j)], [PSK[pk]])
            act(VG[i][:], ps[pk][:], AF.Gelu_apprx_tanh, [PSK[pk]], [("VG", i)])

        v_mm(0)
        for tt_ in range(NTT):
            j = tt_ // 4
            i = tt_ % 2
            if tt_ + 1 < NTT:
                v_mm(tt_ + 1)
            S.add("dve", lambda e, i=i: e.bn_stats(out=BST[i][:, 0:6], in_=VG[i][:]), [("VG", i)], [("BST", i)])
            S.add("dve", lambda e, i=i: e.bn_aggr(out=BST[i][:, 6:8], in_=BST[i][:, 0:6]), [("BST", i)], [("BST", i)])
            ts(BST[i][:, 7:8], BST[i][:, 7:8], LN_EPS, 0.0, ALU.add, ALU.add, [("BST", i)], [("BST", i)])
            tt(BST[i][:, 7:8], BST[i][:, 7:8], NEGH[:, 0:1], ALU.pow, [("BST", i), "NEGH"], [("BST", i)], eng="pool")
            ts(VG[i][:], VG[i][:], BST[i][:, 6:7], BST[i][:, 7:8], ALU.subtract, ALU.mult, [("VG", i), ("BST", i)], [("VG", i)])
            tt(VN[i][:], VG[i][:], SGUG[:], ALU.mult, [("VG", i), "SGUG"], [("VN", i)])
            pv = 4 + tt_ % 2
            for gi in range(4):
                mm(ps[pv][:, gi * 128:(gi + 1) * 128], VN[i][:, gi * 128:(gi + 1) * 128], WST[:, gi, :], True, False,
                   [("VN", i), "WST"], [PSK[pv]])
                mm(ps[pv][:, gi * 128:(gi + 1) * 128], ONESB[0:1, :], SGUB[0:1, gi * 128:(gi + 1) * 128], False, True,
                   ["ONESB", "SGUB"], [PSK[pv]])
            uv = T1[:, :, tt_ * 128:(tt_ + 1) * 128]
            tt(uv, uv, ps[pv][:].rearrange("p (g n) -> p g n", g=4), ALU.mult,
               [PSK[pv]] + [("T1", gi, j) for gi in range(4)], [("T1", gi, j) for gi in range(4)])
        tap("uv_%d" % l, T1[:], [128, 4, NTOK], BF16, [("T1", gi, j) for gi in range(4) for j in range(NJ)])
        merge_phase(l, 1, WSGU, 4, lambda k, j: T1[:, k, j * 512:(j + 1) * 512], lambda k, j: [("T1", k, j)], False)

    def phase_C(l):
        new_phase()
        QK = aalloc("QK", [128, 8, NTOK], BF16)
        SF = aalloc("SF", [128, 2, 4, 256], F32)
        SB = aalloc("SB", [128, 2, 4, 256], BF16)
        ZARENA = acur[0]
        LRT = aalloc("LRT", [32, NTOK], BF16)
        S.dma(W2P[:], W2P_d[l], writes=["W2P"], q="pool")
        S.dma(GLB[:], GLB_d[l], writes=["GLB"], q="pool")
        S.dma(GNG[:], GNG_d[l], writes=["GNG"])
        STG = [aalloc("STG", [128, 512], BF16) for _ in range(3)]
        EZ = [aalloc("EZ", [128, 512], F32) for _ in range(2)]
        ec = [0]

        def compute_qk(bi, h):
            s, keys = h
            for j in range(NJ):
                pk = 2 + ec[0] % 4
                ec[0] += 1
                for kc in range(8):
                    mm(ps[pk][:], RING[:, s, kc * 128:(kc + 1) * 128], H[:, kc, j * 512:(j + 1) * 512], kc == 0, kc == 7,
                       keys + [("H", kc, j)], [PSK[pk]])
                dst = QK[:, bi, j * 512:(j + 1) * 512]
                if bi < 4:
                    act(dst, ps[pk][:], AF.Copy, [PSK[pk]], [("QK", bi, j)], scale=128.0 ** -0.5)
                else:
                    cp(dst, ps[pk][:], [PSK[pk]], [("QK", bi, j)])
        stream(list(range(8)), lambda bi: wload1(WIN[l, B_Q + bi]), compute_qk, 4)
        s, keys = wload1(WIN[l, B_LR])
        for j in range(NJ):
            pk = 2 + j % 4
            for kc in range(8):
                mm(ps[pk][0:32, :], RING[:, s, kc * 128:kc * 128 + 32], H[:, kc, j * 512:(j + 1) * 512], kc == 0, kc == 7,
                   keys + [("H", kc, j)], [PSK[pk]])
            cp(LRT[:, j * 512:(j + 1) * 512], ps[pk][0:32, :], [PSK[pk]], [("LRT", j)])
        sk = [0]
        for grp in range(5):
            blk0 = B_K + grp * 4
            s0, gkeys = ring_load(None, 4, align=4)
            for gi in range(4):
                S.dma(RING[:, s0 + gi, :], WIN[l, blk0 + gi], writes=[("ring", s0 + gi)], q="pool")
            for tt_ in range(NTT):
                j = tt_ // 4
                pk = 2 + tt_ % 4
                for kc in range(8):
                    mm(ps[pk][:], H[:, kc, tt_ * 128:(tt_ + 1) * 128], RING[:, s0:s0 + 4, kc * 128:(kc + 1) * 128],
                       kc == 0, kc == 7, gkeys + [("H", kc, j)], [PSK[pk]])
                i = sk[0] % 3
                sk[0] += 1
                if grp >= 3:
                    act(STG[i][:], ps[pk][:], AF.Silu, [PSK[pk]], [("STG", i)])
                elif tt_ % 2 == 0:
                    act(STG[i][:], ps[pk][:], AF.Copy, [PSK[pk]], [("STG", i)])
                else:
                    cp(STG[i][:], ps[pk][:], [PSK[pk]], [("STG", i)])
                S.dma(KVR[tt_, :, grp * 512:(grp + 1) * 512], STG[i][:], reads=[("STG", i)], writes=[("KVR", tt_, grp)])
        for tt_ in range(NTT):
            j = tt_ // 4
            for d in range(2):
                pk = 2 + (2 * tt_ + d) % 4
                mm(ps[pk][:], LRT[:, tt_ * 128:(tt_ + 1) * 128], W2P[:, d, :], True, False, [("LRT", j), "W2P"], [PSK[pk]])
                mm(ps[pk][:], ONESB[0:1, :], GLB[0:1, d * 512:(d + 1) * 512], False, True, ["ONESB", "GLB"], [PSK[pk]])
                e = (2 * tt_ + d) % 2
                act(EZ[e][:], ps[pk][:], AF.Exp, [PSK[pk]], [("EZ", e)], scale=-1.0)
                i = sk[0] % 3
                sk[0] += 1
                act(STG[i][:], EZ[e][:], AF.Ln, [("EZ", e)], [("STG", i)], bias=1.0)
                S.dma(SPD[tt_, :, d, :], STG[i][:], reads=[("STG", i)], writes=[("SPD", tt_, d)])

        S.fence(DUMMY[:, 0:1])
        acur[0] = ZARENA
        W = {}
        for d in range(2):
            W[d] = dict(
                KV=[aalloc("KV", [128, 1536], BF16) for _ in range(2)], SPT=[aalloc("SPT", [128, 512], BF16) for _ in range(2)],
                KD=aalloc("KD", [128, 512], BF16), EKD=aalloc("EKD", [128, 512], BF16),
                EG=aalloc("EG", [128, 4, 128], F32), ENG=aalloc("ENG", [128, 4, 128], BF16),
                QG=aalloc("QG", [128, 4, 128], BF16), KG=aalloc("KG", [128, 4, 128], BF16),
                ATM=aalloc("ATM", [128, 4, 128], BF16), OS=[aalloc("OS", [128, 256], F32) for _ in range(2)])

        def load_kv(d, tt_, p):
            w = W[d]
            S.dma(w["KV"][p][:], KVR[tt_, :, 0:1536], reads=[("KVR", tt_, g_) for g_ in range(3)], writes=[("KV", d, p)])
            S.dma(w["SPT"][p][:], SPD[tt_, :, d, :], reads=[("SPD", tt_, d)], writes=[("SPT", d, p)])

        def stage1(d, tt_, p):
            w = W[d]
            j = tt_ // 4
            tok = tt_ * 128
            kv, spt = ("KV", d, p), ("SPT", d, p)
            SPTt, KVt = w["SPT"][p], w["KV"][p]
            pD, pG = 0, 1
            mm(ps[pD][:], TMS[:, d, 1, :], SPTt[:], True, True, ["TMS", spt], [PSK[pD]])
            for h in range(4):
                mm(ps[pG][:, h * 128:(h + 1) * 128], SPTt[:, h * 128:(h + 1) * 128], TMS[:, d, 0, :], True, True,
                   ["TMS", spt], [PSK[pG]])
            gv = ps[pG][:].rearrange("p (h n) -> p h n", h=4)
            act(w["EG"][:], gv, AF.Exp, [PSK[pG]], [("EG", d)])
            act(w["ENG"][:], gv, AF.Exp, [PSK[pG]], [("ENG", d)], scale=-1.0)
            act(w["EKD"][:], ps[pD][:], AF.Exp, [PSK[pD]], [("EKD", d)])
            tt(w["QG"][:], QK[:, 0:4, tok:tok + 128], w["EG"][:], ALU.mult, [("QK", b_, j) for b_ in range(4)] + [("EG", d)],
               [("QG", d)])
            tt(w["KG"][:], QK[:, 4:8, tok:tok + 128], w["ENG"][:], ALU.mult, [("QK", b_, j) for b_ in range(4, 8)] + [("ENG", d)],
               [("KG", d)])
            tt(w["KD"][:], KVt[:, 0:512], w["EKD"][:], ALU.mult, [kv, ("EKD", d)], [("KD", d)])

        def stage2(d, tt_, p):
            w = W[d]
            pA = 0
            for h in range(4):
                mm(ps[pA][:, h * 128:(h + 1) * 128], w["KG"][:, h, :], w["QG"][:, h, :], True, True,
                   [("KG", d), ("QG", d)], [PSK[pA]])
            tt(w["ATM"][:], ps[pA][:].rearrange("p (h n) -> p h n", h=4), MASKA[:, d], ALU.mult, [PSK[pA], "MASKA"], [("ATM", d)])

        def chunk_step(d, h, c, p):
            w = W[d]
            kv = ("KV", d, p)
            KVt = w["KV"][p]
            pO = 3 + h
            r0, r1 = c * 64, (c + 1) * 64
            mm(ps[pO][r0:r1, 0:256], w["QG"][:, h, r0:r1], SB[:, d, h, :], False, True,
               [("QG", d), ("SB", d, h)], [PSK[pO]])
            pS = 2 if h % 2 == 0 else 7
            mm(ps[pS][:, 0:256], w["KD"][r0:r1, h * 128:(h + 1) * 128],
               KVt[r0:r1, 512 + h * 256:512 + (h + 1) * 256], True, True, [("KD", d), kv], [PSK[pS]])
            lend = (c * 64 + 63) if d == 0 else c * 64
            stt(SF[:, d, h, :], SF[:, d, h, :], w["EG"][:, h, lend:lend + 1], ps[pS][:, 0:256], ALU.mult, ALU.add,
                [("SF", d, h), ("EG", d), PSK[pS]], [("SF", d, h)])
            act(SB[:, d, h, :], SF[:, d, h, :], AF.Copy, [("SF", d, h)], [("SB", d, h)])

        def part1(d, tt_, p):
            w = W[d]
            corder = (0, 1) if d == 0 else (1, 0)
            for h in range(4):
                mm(ps[3 + h][:, 0:256], w["ATM"][:, h, :], w["KV"][p][:, 512 + h * 256:512 + (h + 1) * 256], True, False,
                   [("ATM", d), ("KV", d, p)], [PSK[3 + h]])
                chunk_step(d, h, corder[0], p)

        def part2(d, tt_, p):
            w = W[d]
            corder = (0, 1) if d == 0 else (1, 0)
            for h in range(4):
                chunk_step(d, h, corder[1], p)
                o_ = w["OS"][h % 2]
                if h % 2 == 0:
                    act(o_[:], ps[3 + h][:, 0:256], AF.Copy, [PSK[3 + h]], [("OS", d, h % 2)])
                else:
                    cp(o_[:], ps[3 + h][:, 0:256], [PSK[3 + h]], [("OS", d, h % 2)])
                S.dma(OD[d, tt_, :, h * 256:(h + 1) * 256], o_[:], reads=[("OS", d, h % 2)], writes=[("OD", d, tt_, h)])

        for si, (off, T, latent) in enumerate(SEQS):
            nt = T // 128
            t0 = off // 128
            for d in range(2):
                for h in range(4):
                    if latent:
                        S.dma(SF[:, d, h, :], st0[l, d, h], writes=[("SF", d, h)])
                    else:
                        S.add("dve", lambda e, d=d, h=h: e.memset(SF[:, d, h, :], 0.0), [], [("SF", d, h)])
                    act(SB[:, d, h, :], SF[:, d, h, :], AF.Copy, [("SF", d, h)], [("SB", d, h)])
            seq = []
            for i in range(nt):
                seq.append((0, t0 + i, i % 2))
                seq.append((1, t0 + nt - 1 - i, i % 2))
            n_ = len(seq)
            load_kv(*seq[0])
            load_kv(*seq[1])
            stage1(*seq[0])
            stage2(*seq[0])
            for i, (d, t_, p) in enumerate(seq):
                if i + 2 < n_:
                    load_kv(*seq[i + 2])
                if i + 1 < n_:
                    stage1(*seq[i + 1])
                part1(d, t_, p)
                if i + 1 < n_:
                    stage2(*seq[i + 1])
                part2(d, t_, p)
            if not latent:
                for d in range(2):
                    for h in range(4):
                        S.dma(nsout[si - 1, l, d, h], SF[:, d, h, :], reads=[("SF", d, h)], writes=[("ns", si, l, d, h)])

        new_phase()
        OGT = aalloc("OGT", [128, 8, NTOK], BF16)
        OF = [aalloc("OF", [128, 1024], F32) for _ in range(3)]
        OB = [aalloc("OB", [128, 1024], F32) for _ in range(3)]
        RT = [aalloc("RT", [128, 1024], BF16) for _ in range(3)]
        RG = [aalloc("RG", [128, 1024], BF16) for _ in range(3)]
        OG = [aalloc("OG", [128, 1024], BF16) for _ in range(3)]
        SQJ = aalloc("SQJ", [128, 256], BF16)
        SS = [aalloc("SS", [128, 8], F32) for _ in range(3)]
        for tt_ in range(NTT):
            i = tt_ % 3
            j = tt_ // 4
            S.dma(OF[i][:], OD[0, tt_], reads=[("OD", 0, tt_, h_) for h_ in range(4)], writes=[("OF", i)])
            S.dma(OB[i][:], OD[1, tt_], reads=[("OD", 1, tt_, h_) for h_ in range(4)], writes=[("OB", i)])
            S.dma(RT[i][:], KVR[tt_, :, 1536:2560], reads=[("KVR", tt_, 3), ("KVR", tt_, 4)], writes=[("RT", i)])
            tt(OF[i][:], OF[i][:], OB[i][:], ALU.add, [("OF", i), ("OB", i)], [("OF", i)])
            for h in range(4):
                act(SQJ[:], OF[i][:, h * 256:(h + 1) * 256], AF.Square, [("OF", i)], ["SQJ", ("SS", i)],
                    accum_out=SS[i][:, h:h + 1])
            act(SS[i][:, 4:8], SS[i][:, 0:4], AF.Sqrt, [("SS", i)], [("SS", i)], bias=LN_EPS, scale=1.0 / 256)
            S.add("dve", lambda e, i=i: e.reciprocal(out=SS[i][:, 4:8], in_=SS[i][:, 4:8]), [("SS", i)], [("SS", i)])
            for h in range(4):
                tt(RG[i][:, h * 256:(h + 1) * 256], RT[i][:, h * 256:(h + 1) * 256], GNG[:], ALU.mult, [("RT", i), "GNG"],
                   [("RG", i)], eng="pool")
            for h in range(4):
                stt(OG[i][:, h * 256:(h + 1) * 256], OF[i][:, h * 256:(h + 1) * 256], SS[i][:, 4 + h:5 + h],
                    RG[i][:, h * 256:(h + 1) * 256], ALU.mult, ALU.mult, [("OF", i), ("SS", i), ("RG", i)], [("OG", i)])
            pk = 2 + tt_ % 2
            pb = ps[pk][:].bitcast(BF16)
            for c in range(8):
                S.add("pe", lambda e, c=c, pb=pb, i=i: e.transpose(pb[:, c * 128:(c + 1) * 128], OG[i][:, c * 128:(c + 1) * 128], IDB[:]),
                      [("OG", i), "IDB"], [PSK[pk]])
            dst = OGT[:, :, tt_ * 128:(tt_ + 1) * 128]
            src = pb[:, 0:1024].rearrange("p (c n) -> p c n", c=8)
            if tt_ % 2 == 0:
                act(dst, src, AF.Copy, [PSK[pk]], [("OGT", tt_)])
            else:
                cp(dst, src, [PSK[pk]], [("OGT", tt_)])
        tap("ogt_%d" % l, OGT[:], [128, 8, NTOK], BF16, [("OGT", t_) for t_ in range(NTT)])
        merge_phase(l, 2, WGLA, 8, lambda k, j: OGT[:, k, j * 512:(j + 1) * 512],
                    lambda k, j: [("OGT", 4 * j + q_) for q_ in range(4)], False)
        tap("m_%d" % l, M[:], [128, 8, NTOK], BF16, [("M", c, j) for c in range(8) for j in range(NJ)])

    PADL = 65
    SEGOFF = [PADL, PADL + 2048 + 66, PADL + 2048 + 66 + 256 + 2]
    AW = SEGOFF[2] + 256 + 2

    def phase_FU(l):
        new_phase()
        AV = [aalloc("AV", [128, 3, AW], BF16) for _ in range(2)]
        DG = [aalloc("DG", [128, 9, 128], BF16) for _ in range(2)]
        CG = [aalloc("CG", [128, 512], BF16) for _ in range(2)]
        PC = [aalloc("PC", [128, NTOK], BF16) for _ in range(2)]
        CM = aalloc("CM", [128, 2, 2048], BF16)
        S.add("pool", lambda e: e.memset(CM[:], 1.0), [], ["CM"])
        cmv = CM[:].rearrange("p a (r c) -> p a r c", c=64)
        S.add("pool", lambda e: e.memset(cmv[:, 0, :, 63:64], 0.0), ["CM"], ["CM"])
        S.add("pool", lambda e: e.memset(cmv[:, 1, :, 0:1], 0.0), ["CM"], ["CM"])
        for a in range(2):
            S.add("pool", lambda e, a=a: e.memset(AV[a][:], 0.0), [], [("AV", a)])
        cg = [0]

        def loader(c):
            return wload1(WUP[l, c]), wload1(WUP[l, NC_FF + c])

        pend = []

        def compute(c, h):
            (sa, ka), (sg, kg) = h
            a = c % 2
            A = AV[a]
            for tap_i in range(9):
                ts(DG[a][:, tap_i, :], IDB[:], parap(l, P_WDW, tap_i * NC_FF + c), 0.0, ALU.mult, ALU.add, ["IDB", "PAR"],
                   [("DG", a)], eng="pool")
            for j in range(NJ):
                pk = 1 + j % 3
                for kc in range(8):
                    mm(ps[pk][:], RING[:, sa, kc * 128:(kc + 1) * 128], H[:, kc, j * 512:(j + 1) * 512], kc == 0, kc == 7,
                       ka + [("H", kc, j)], [PSK[pk]])
                if j < 4:
                    o = SEGOFF[0] + j * 512
                    act(A[:, 1, o:o + 512], ps[pk][:], AF.Copy, [PSK[pk]], [("AV", a)])
                    tt(A[:, 0, o:o + 512], ps[pk][:], CM[:, 0, j * 512:(j + 1) * 512], ALU.mult, [PSK[pk], "CM"], [("AV", a)])
                    tt(A[:, 2, o:o + 512], ps[pk][:], CM[:, 1, j * 512:(j + 1) * 512], ALU.mult, [PSK[pk], "CM"], [("AV", a)])
                else:
                    for s_ in range(2):
                        o = SEGOFF[1 + s_]
                        act(A[:, 1, o:o + 256], ps[pk][:, s_ * 256:(s_ + 1) * 256], AF.Copy, [PSK[pk]], [("AV", a)])
            if l + 1 < n_layers:
                for (jb, hb) in pend:
                    mod_block(l + 1, jb, hb)
            for j in range(NJ):
                pg = 4 + j % 2
                pc = 6 + j % 2
                for kc in range(8):
                    mm(ps[pg][:], RING[:, sg, kc * 128:(kc + 1) * 128], H[:, kc, j * 512:(j + 1) * 512], kc == 0, kc == 7,
                       kg + [("H", kc, j)], [PSK[pg]])
                if j < 4:
                    o = SEGOFF[0] + j * 512
                    for tap_i in range(9):
                        dy, dx = tap_i // 3, tap_i % 3
                        sh = (dy - 1) * 64 + (dx - 1)
                        mm(ps[pc][:], DG[a][:, tap_i, :], A[:, dx, o + sh:o + sh + 512], tap_i == 0, tap_i == 8,
                           [("DG", a), ("AV", a)], [PSK[pc]])
                else:
                    for s_ in range(2):
                        o = SEGOFF[1 + s_]
                        for dx in range(3):
                            mm(ps[pc][:, s_ * 256:(s_ + 1) * 256], DG[a][:, 3 + dx, :], A[:, 1, o + dx - 1:o + dx - 1 + 256],
                               dx == 0, dx == 2, [("DG", a), ("AV", a)], [PSK[pc]])
                i = cg[0] % 2
                cg[0] += 1
                act(CG[i][:], ps[pc][:], AF.Gelu_apprx_tanh, [PSK[pc], "PAR"], [("CG", i)], bias=parap(l, P_BDW, c))
                tt(PC[a][:, j * 512:(j + 1) * 512], ps[pg][:], CG[i][:], ALU.mult, [PSK[pg], ("CG", i)], [("PC", a, j)])
            S.dma(PD[c], PC[a][:], reads=[("PC", a, j) for j in range(NJ)], writes=[("PD", c)])
            if l + 1 < n_layers:
                pend[:] = [(jb, wload1(WADA[l + 1, jb])) for jb in range(3 * c, min(48, 3 * c + 3))]
        stream(list(range(NC_FF)), loader, compute, 2)
        if l + 1 < n_layers:
            for (jb, hb) in pend:
                mod_block(l + 1, jb, hb)
            mod_finish(l + 1)

    phase_mod()
    if stop_after != "MOD":
        phase_L0()
    if DBG > 5:
        tap("h0", H[:], [128, 8, NTOK], BF16, [("H", c, j) for c in range(8) for j in range(NJ)])
    tap("mod", MOD[:, 0:n_layers], [128, n_layers, 48, 2], F32, [("MOD", l_) for l_ in range(n_layers)])
    done = stop_after in ("L0", "MOD")
    for l in range(n_layers):
        if done:
            break
        for name, fn in (("A", phase_A), ("B", phase_B), ("C", phase_C), ("R1", lambda l: phase_R(l, 1)),
                         ("FU", phase_FU), ("R2", lambda l: phase_R(l, 2))):
            fn(l)
            if stop_after == "%s%d" % (name, l):
                done = True
                break
        tap("h_%d" % (l + 1), H[:], [128, 8, NTOK], BF16, [("H", c, j) for c in range(8) for j in range(NJ)])
    tap("modend", MOD[:, 0:n_layers], [128, n_layers, 48, 2], F32, [("MOD", l_) for l_ in range(n_layers)])
    new_phase()
    S.emit()
    return nc, tap_out


def _tile_w(w):
    K, N = w.shape
    return np.ascontiguousarray(w.reshape(K // 128, 128, N // 128, 128).transpose(2, 1, 0, 3).reshape(N // 128, 128, K))


def _fm(v):
    n = v.shape[-1] // 128
    return np.moveaxis(v.reshape(v.shape[:-1] + (n, 128)), -1, 0)


def _consts():
    bf = ml_dtypes.bfloat16
    c = {}
    c["IDB"] = np.eye(128, dtype=np.float32).astype(bf)
    k = np.arange(128)
    ang = 2 * np.pi * np.outer(k, k) / 128.0
    c["CSC"] = np.concatenate([np.cos(ang), np.sin(ang)], axis=1).astype(np.float32).astype(bf)
    t = np.arange(256)
    a = 2 * np.pi * ((np.outer(t, t)) % 256) / 256.0
    dp = np.stack([np.cos(a), -np.sin(a)], 0).reshape(2, 2, 128, 256)
    c["DFTP"] = np.ascontiguousarray(dp.transpose(2, 0, 1, 3)).astype(np.float32).astype(bf)
    t = np.arange(2048, dtype=np.int64)
    a = 2 * np.pi * ((np.outer(t, t)) % 2048) / 2048.0
    dl = np.stack([np.cos(a), -np.sin(a)], 0).reshape(2, 16, 128, 8, 256)
    c["DFTL"] = np.ascontiguousarray(dl.transpose(3, 0, 2, 1, 4)).astype(np.float32).astype(bf)
    m = np.arange(128)[:, None]
    l = np.arange(128)[None, :]
    same = (m // 64) == (l // 64)
    tms = np.zeros((128, 2, 2, 128), np.float32)
    tms[:, 0, 0] = np.where(same & (m <= l), -1.0 / 16, 0)
    tms[:, 0, 1] = np.where(same & (m > l), -1.0 / 16, 0)
    tms[:, 1, 0] = np.where(same & (m >= l), -1.0 / 16, 0)
    tms[:, 1, 1] = np.where(same & (m < l), -1.0 / 16, 0)
    c["TMS"] = tms.astype(bf)
    ma = np.zeros((128, 2, 4, 128), np.float32)
    ma[:, 0] = np.where(same & (m <= l), 1.0, 0)[:, None, :]
    ma[:, 1] = np.where(same & (m >= l), 1.0, 0)[:, None, :]
    c["MASKA"] = ma.astype(bf)
    return c


def prep_shared(w_in, b_merge, w_fft_out, sgu_g, sgu_ws, sgu_b, w_sgu_out, gla_w2, gla_b, gla_norm_g, w_gla_out, w_o,
                ln1_g, ln1_b, w_ada, b_ada, w_up, w_dw, b_dw, w_down, ln2_g, ln2_b):
    L = DEPTH
    f = lambda a: np.asarray(a, dtype=np.float32)
    w_in, w_ada, w_up, w_down = f(w_in), f(w_ada), f(w_up), f(w_down)
    sh = {}
    wext = np.concatenate([w_in[:, :, :4640], np.zeros((L, D, 96), np.float32), w_in[:, :, 4640:]], axis=2)
    sh["WIN"] = np.stack([_tile_w(wext[l]) for l in range(L)])
    sh["WFFT"] = np.stack([_tile_w(f(w_fft_out)[l]) for l in range(L)])
    sh["WSGU"] = np.stack([_tile_w(f(w_sgu_out)[l]) for l in range(L)])
    sh["WGLA"] = np.stack([_tile_w(f(w_gla_out)[l]) for l in range(L)])
    sh["WO"] = np.stack([_tile_w(f(w_o)[l]) for l in range(L)])
    sh["WUP"] = np.stack([_tile_w(w_up[l]) for l in range(L)])
    sh["WDOWN"] = np.stack([_tile_w(w_down[l]) for l in range(L)])
    sh["WADA"] = np.stack([_tile_w(w_ada[l]) for l in range(L)])
    par = np.zeros((128, L, PL), np.float32)
    par[:, :, P_BADA:P_BADA + 48] = _fm(f(b_ada))
    par[:, :, P_BM:P_BM + 24] = _fm(f(b_merge)).reshape(128, L, 24)
    lnp = np.stack([f(ln1_g), f(ln1_b), f(ln2_g), f(ln2_b)], axis=1)
    par[:, :, P_LN:P_LN + 32] = _fm(lnp).reshape(128, L, 32)
    par[:, :, P_BDW:P_BDW + 22] = _fm(f(b_dw))
    par[:, :, P_WDW:P_WDW + 198] = _fm(f(w_dw).reshape(L, 9, D_FF)).reshape(128, L, 198)
    sh["PAR"] = np.ascontiguousarray(par.reshape(128, L * PL))
    sh["SGUG"] = np.ascontiguousarray(np.broadcast_to(f(sgu_g)[:, None, :], (L, 128, 512)))
    sh["GNG"] = np.ascontiguousarray(np.broadcast_to(f(gla_norm_g)[:, None, :], (L, 128, 256)))
    sh["WST"] = np.ascontiguousarray(f(sgu_ws).transpose(0, 3, 1, 2))
    w2p = np.zeros((L, 32, 2, 512), np.float32)
    for d in range(2):
        w2p[:, d * 16:(d + 1) * 16, d, :] = f(gla_w2)[:, d]
    sh["W2P"] = w2p
    sh["GLB"] = np.ascontiguousarray(f(gla_b).reshape(L, 1, 1024))
    sh["SGUB"] = np.ascontiguousarray(f(sgu_b).reshape(L, 1, 512))
    sh.update(_consts())
    return sh


def _to_fm_tiles(x_tok):
    return np.ascontiguousarray(x_tok.reshape(NJ, 512, 8, 128).transpose(0, 3, 2, 1))


def _from_fm_tiles(y):
    return y.transpose(0, 3, 2, 1).reshape(NTOK, D)


def prep_core(i, x_prompt, x_sample, state_gla, c, c_ctx):
    xt = np.concatenate([x_sample[i], x_prompt[2 * i], x_prompt[2 * i + 1]], axis=0).astype(np.float32)
    m = {"xin": _to_fm_tiles(xt)}
    cc = np.stack([np.asarray(c[i], np.float32), np.asarray(c_ctx, np.float32)], axis=-1)
    m["cT"] = np.ascontiguousarray(cc.reshape(8, 128, 2).transpose(1, 0, 2))
    m["st0"] = np.ascontiguousarray(np.asarray(state_gla[i], np.float32))
    return m


_CACHE = {}


def kernel(x_prompt, x_sample, state_gla, c, c_ctx, **w):
    x_prompt, x_sample = np.asarray(x_prompt), np.asarray(x_sample)
    state_gla, c, c_ctx = np.asarray(state_gla), np.asarray(c), np.asarray(c_ctx)
    sh = prep_shared(**{k: np.asarray(v) for k, v in w.items()})
    if "nc" not in _CACHE:
        _CACHE["nc"] = build()[0]
    nc = _CACHE["nc"]
    in_maps = []
    for i in range(8):
        m = dict(sh)
        m.update(prep_core(i, x_prompt, x_sample, state_gla, c, c_ctx))
        in_maps.append(m)
    res = run_bass_kernel_spmd(nc, in_maps, core_ids=list(range(8)))
    y_prompt = np.zeros((16, 256, D), np.float32)
    y_sample = np.zeros((8, 2048, D), np.float32)
    ns = np.zeros((16, DEPTH, 2, 4, 128, 256), np.float32)
    for i in range(8):
        r = res.results[i]
        y = _from_fm_tiles(np.asarray(r["yout"], np.float32))
        y_sample[i] = y[:2048]
        y_prompt[2 * i] = y[2048:2304]
        y_prompt[2 * i + 1] = y[2304:2560]
        nso = np.asarray(r["nsout"], np.float32)
        ns[2 * i] = nso[0]
        ns[2 * i + 1] = nso[1]
    return (y_prompt, y_sample, ns)
```

```python
import contextlib
import math
import numpy as np
import ml_dtypes
import concourse.bass as bass
import concourse.mybir as mybir
from concourse.bass_utils import run_bass_kernel_spmd

F32 = mybir.dt.float32
BF16 = mybir.dt.bfloat16
AF = mybir.ActivationFunctionType
ALU = mybir.AluOpType

D = 1024
DEPTH = 4
NTOK = 2560
NTT = 20
NJ = 5
D_FF = 2816
NC_FF = 22
LN_EPS = 1e-5
ALPHA = (2.0 * DEPTH) ** 0.25
SEQS = [(0, 2048, True), (2048, 256, False), (2304, 256, False)]
PL = 48 + 24 + 32 + 22 + 198
P_BADA, P_BM, P_LN, P_BDW, P_WDW = 0, 48, 72, 104, 126
B_AF, B_U, B_V, B_Q, B_K, B_VV, B_R, B_LR, B_GM = 0, 4, 8, 12, 16, 20, 28, 36, 37

COMPUTE = ("pe", "act", "dve", "pool")
NDMASEM = 6


class _I:
    __slots__ = ("eng", "fn", "deps", "signal", "sigval", "is_dma", "dsem")

    def __init__(self, eng, fn, is_dma):
        self.eng = eng
        self.fn = fn
        self.deps = []
        self.signal = False
        self.sigval = 0
        self.is_dma = is_dma
        self.dsem = None


class Sched:
    def __init__(self, nc):
        self.nc = nc
        self.q = {e: [] for e in ("pe", "act", "dve", "pool", "sp")}
        self.lastw = {}
        self.readers = {}
        self.arena_names = {"M", "ps"}
        self.cur_fence = None
        self.n = 0

    def _name(self, k):
        return k[0] if isinstance(k, tuple) else k

    def add(self, eng, fn, reads=(), writes=(), dma=False):
        ins = _I(eng, fn, dma)
        deps = set()
        for k in list(reads) + list(writes):
            if k not in self.lastw and self.cur_fence is not None and self._name(k) in self.arena_names:
                self.lastw[k] = self.cur_fence
        for r in reads:
            w = self.lastw.get(r)
            if w is not None:
                deps.add(w)
        for w_ in writes:
            w = self.lastw.get(w_)
            if w is not None:
                deps.add(w)
            for rd in self.readers.get(w_, ()):
                deps.add(rd)
        deps.discard(ins)
        for d in deps:
            if d.eng == "pe" and eng == "pe" and not d.is_dma and not dma:
                continue
            d.signal = True
            ins.deps.append(d)
        for r in reads:
            self.readers.setdefault(r, []).append(ins)
        for w_ in writes:
            self.lastw[w_] = ins
            self.readers[w_] = []
        self.q[eng].append(ins)
        self.n += 1
        return ins

    def dma(self, out, in_, reads=(), writes=(), q="sp"):
        return self.add(q, lambda e: e.dma_start(out=out, in_=in_), reads, writes, dma=True)

    def fence(self, dummy_ap):
        keys = [k for k in set(self.lastw) | set(self.readers) if self._name(k) in self.arena_names]
        f = self.add("dve", lambda e: e.memset(dummy_ap, 0.0), reads=[], writes=keys + ["__fence__"])
        for k in keys:
            self.lastw.pop(k, None)
            self.readers.pop(k, None)
        self.cur_fence = f

    def emit(self):
        nc = self.nc
        with contextlib.ExitStack() as es:
            csem = {e: es.enter_context(nc.semaphore("c_" + e)) for e in COMPUTE}
            dsems = {}
            dma_lists = {}
            for qn in self.q:
                lst = [i for i in self.q[qn] if i.is_dma]
                if lst:
                    dsems[qn] = [es.enter_context(nc.semaphore("d_%s_%d" % (qn, k))) for k in range(NDMASEM)]
                    dma_lists[qn] = lst
                    for k, i in enumerate(lst):
                        i.dsem = (dsems[qn][k % NDMASEM], 16 * (k // NDMASEM + 1))
            for e in COMPUTE:
                cnt = 0
                for i in self.q[e]:
                    if not i.is_dma and i.signal:
                        cnt += 1
                        i.sigval = cnt
            block = es.enter_context(nc.Block())

            def run_engine(qn, eng):
                waited = {}
                dcount = 0

                def need(sem, val):
                    key = id(sem)
                    if waited.get(key, 0) >= val:
                        return
                    waited[key] = val
                    eng.wait_ge(sem, val)

                for i in self.q[qn]:
                    if i.is_dma:
                        k = dcount
                        dcount += 1
                        if k >= NDMASEM:
                            prev = dma_lists[qn][k - NDMASEM]
                            need(prev.dsem[0], prev.dsem[1])
                    for d in i.deps:
                        if d.is_dma:
                            need(d.dsem[0], d.dsem[1])
                        else:
                            need(csem[d.eng], d.sigval)
                    h = i.fn(eng)
                    if i.is_dma:
                        h.then_inc(i.dsem[0], 16)
                    elif i.signal:
                        h.then_inc(csem[i.eng], 1)
                if qn in dma_lists:
                    for i in dma_lists[qn][-NDMASEM:]:
                        need(i.dsem[0], i.dsem[1])

            @block.tensor
            def _(e):
                run_engine("pe", e)

            @block.scalar
            def _(e):
                run_engine("act", e)

            @block.vector
            def _(e):
                run_engine("dve", e)

            @block.gpsimd
            def _(e):
                run_engine("pool", e)

            @block.sync
            def _(e):
                run_engine("sp", e)


def build(n_layers=DEPTH, taps=(), stop_after=None):
    nc = bass.Bass("TRN2", target_bir_lowering=False)
    S = Sched(nc)
    L = DEPTH

    def din(name, shape, dtype=F32):
        return nc.dram_tensor(name, list(shape), dtype, kind="ExternalInput").ap()

    def dout(name, shape, dtype=F32):
        return nc.dram_tensor(name, list(shape), dtype, kind="ExternalOutput").ap()

    def dscr(name, shape, dtype):
        return nc.dram_tensor(name, list(shape), dtype).ap()

    xin = din("xin", [NJ, 128, 8, 512])
    cT_d = din("cT", [128, 8, 2])
    st0 = din("st0", [L, 2, 4, 128, 256])
    WIN = din("WIN", [L, 61, 128, 1024])
    WFFT = din("WFFT", [L, 8, 128, 512])
    WSGU = din("WSGU", [L, 8, 128, 512])
    WGLA = din("WGLA", [L, 8, 128, 1024])
    WO = din("WO", [L, 8, 128, 1024])
    WUP = din("WUP", [L, 44, 128, 1024])
    WDOWN = din("WDOWN", [L, 8, 128, 2816])
    WADA = din("WADA", [L, 48, 128, 1024])
    PAR_d = din("PAR", [128, L * PL])
    SGUG_d = din("SGUG", [L, 128, 512])
    GNG_d = din("GNG", [L, 128, 256])
    WST_d = din("WST", [L, 128, 4, 128])
    W2P_d = din("W2P", [L, 32, 2, 512])
    GLB_d = din("GLB", [L, 1, 1024])
    SGUB_d = din("SGUB", [L, 1, 512])
    IDB_d = din("IDB", [128, 128], BF16)
    CSC_d = din("CSC", [128, 256], BF16)
    DFTP_d = din("DFTP", [128, 2, 2, 256], BF16)
    TMS_d = din("TMS", [128, 2, 2, 128], BF16)
    MASKA_d = din("MASKA", [128, 2, 4, 128], BF16)
    DFTL_d = din("DFTL", [8, 2, 128, 16, 256], BF16)
    yout = dout("yout", [NJ, 128, 8, 512])
    nsout = dout("nsout", [2, L, 2, 4, 128, 256])
    XD = dscr("XD", [NJ, 128, 8, 512], F32)
    KVR = dscr("KVR", [NTT, 128, 2560], BF16)
    SPD = dscr("SPD", [NTT, 128, 2, 512], BF16)
    OD = dscr("OD", [2, NTT, 128, 1024], F32)
    PD = dscr("PD", [NC_FF, 128, NTOK], BF16)
    tap_out = {}

    cur = [16512]
    LIMIT = 229344

    def salloc(name, shape, dtype):
        nb = int(np.prod(shape[1:])) * (4 if dtype == F32 else 2)
        nb = (nb + 31) // 32 * 32
        off = cur[0]
        cur[0] += nb
        assert cur[0] <= LIMIT, (name, cur[0])
        return nc.alloc_sbuf_tensor_at(name, list(shape), dtype, offset=off)

    IDB = salloc("IDB", [128, 128], BF16)
    ONESB = salloc("ONESB", [128, 128], BF16)
    CSC = salloc("CSC", [128, 256], BF16)
    DFTP = salloc("DFTP", [128, 2, 2, 256], BF16)
    TMS = salloc("TMS", [128, 2, 2, 128], BF16)
    MASKA = salloc("MASKA", [128, 2, 4, 128], BF16)
    PAR = salloc("PAR", [128, L * PL], F32)
    MOD = salloc("MOD", [128, L, 48, 2], F32)
    CT = salloc("CT", [128, 8, 2], F32)
    SCT = salloc("SCT", [128, 8, 2], F32)
    SCB = salloc("SCB", [128, 8, 2], BF16)
    DUMMY = salloc("DUMMY", [128, 8], F32)
    NEGH = salloc("NEGH", [128, 8], F32)
    SGUG = salloc("SGUG", [128, 512], F32)
    GNG = salloc("GNG", [128, 256], F32)
    WST = salloc("WST", [128, 4, 128], BF16)
    W2P = salloc("W2P", [32, 2, 512], BF16)
    GLB = salloc("GLB", [1, 1024], BF16)
    SGUB = salloc("SGUB", [1, 512], BF16)
    H = salloc("H", [128, 8, NTOK], BF16)
    M = salloc("M", [128, 8, NTOK], BF16)
    M_OFF = cur[0] - 8 * NTOK * 2
    RING = salloc("RING", [128, 8, 1024], BF16)
    ARENA0 = cur[0]
    acur = [ARENA0]
    seqno = [0]

    def aalloc(name, shape, dtype):
        nb = int(np.prod(shape[1:])) * (4 if dtype == F32 else 2)
        nb = (nb + 31) // 32 * 32
        off = acur[0]
        acur[0] += nb
        assert acur[0] <= LIMIT, (name, acur[0] - LIMIT)
        S.arena_names.add(name)
        seqno[0] += 1
        return nc.alloc_sbuf_tensor_at("%s_%d" % (name, seqno[0]), list(shape), dtype, offset=off)

    def new_phase():
        S.fence(DUMMY[:, 0:1])
        acur[0] = ARENA0

    ps = [nc.alloc_psum_tensor("ps%d" % i, [128, 512], F32) for i in range(8)]
    PSK = [("ps", i) for i in range(8)]

    def mm(out, lhsT, rhs, st, sp, r, w):
        S.add("pe", lambda e: e.matmul(out, lhsT, rhs, start=st, stop=sp), r, w)

    def act(out, in_, func, r, w, **kw):
        S.add("act", lambda e: e.activation(out=out, in_=in_, func=func, **kw), r, w)

    def tt(out, in0, in1, op, r, w, eng="dve"):
        S.add(eng, lambda e: e.tensor_tensor(out=out, in0=in0, in1=in1, op=op), r, w)

    def ts(out, in0, s1, s2, op0, op1, r, w, eng="dve"):
        S.add(eng, lambda e: e.tensor_scalar(out=out, in0=in0, scalar1=s1, scalar2=s2, op0=op0, op1=op1), r, w)

    def stt(out, in0, sc, in1, op0, op1, r, w):
        S.add("dve", lambda e: e.scalar_tensor_tensor(out=out, in0=in0, scalar=sc, in1=in1, op0=op0, op1=op1), r, w)

    def cp(out, in_, r, w, eng="dve"):
        S.add(eng, lambda e: e.tensor_copy(out=out, in_=in_), r, w)

    def tap(name, ap, shape, dtype, reads):
        if name in taps:
            t = dout("tap_" + name, shape, dtype)
            tap_out[name] = t
            if len(shape) == 3 and shape[1] <= 8:
                for i_ in range(min(shape[1], int(os.environ.get("K_TAPN", "8")))):
                    S.dma(t[:, i_], ap[:, i_], reads=reads, writes=[("tap", name, i_)])
            else:
                S.dma(t, ap, reads=reads, writes=[("tap", name)])

    rptr = [0]

    def ring_load(src, nslots, align=1, nelem=1024):
        p = (rptr[0] + align - 1) // align * align
        if p + nslots > 8:
            p = 0
        rptr[0] = (p + nslots) % 8
        keys = [("ring", s) for s in range(p, p + nslots)]
        return p, keys

    def wload1(src_ap, nelem=1024):
        s, keys = ring_load(src_ap, 1)
        S.dma(RING[:, s, 0:nelem], src_ap, writes=keys, q="pool")
        return s, keys

    def stream(items, loader, compute, depth):
        hs = {}
        n = len(items)
        for i in range(min(depth, n)):
            hs[i] = loader(items[i])
        for i in range(n):
            compute(items[i], hs.pop(i))
            if i + depth < n:
                hs[i + depth] = loader(items[i + depth])

    def grp_of(j):
        return 0 if j < 4 else 1

    def modap(l, slot, c, g):
        return MOD[:, l, slot * 8 + c, g:g + 1]

    def parap(l, off, idx):
        return PAR[:, l * PL + off + idx:l * PL + off + idx + 1]

    S.dma(IDB[:], IDB_d, writes=["IDB"])
    S.dma(CSC[:], CSC_d, writes=["CSC"])
    S.dma(DFTP[:], DFTP_d, writes=["DFTP"])
    S.dma(TMS[:], TMS_d, writes=["TMS"])
    S.dma(MASKA[:], MASKA_d, writes=["MASKA"])
    S.dma(PAR[:], PAR_d, writes=["PAR"])
    S.dma(CT[:], cT_d, writes=["CT"])
    S.add("dve", lambda e: e.memset(ONESB[:], 1.0), writes=["ONESB"])
    S.add("dve", lambda e: e.memset(NEGH[:], -0.5), writes=["NEGH"])
    act(SCT[:], CT[:], AF.Silu, ["CT"], ["SCT"])
    cp(SCB[:], SCT[:], ["SCT"], ["SCB"])

    def mod_block(l, j, h):
        s_, keys = h
        for kc in range(8):
            mm(ps[0][:, 2 * j:2 * j + 2], RING[:, s_, kc * 128:(kc + 1) * 128], SCB[:, kc, :], kc == 0, kc == 7,
               keys + ["SCB"], [PSK[0]])

    def mod_finish(l):
        pv = ps[0][:, 0:96].rearrange("p (j g) -> p j g", g=2)
        for g in range(2):
            tt(MOD[:, l, :, g], pv[:, :, g], PAR[:, l * PL + P_BADA:l * PL + P_BADA + 48], ALU.add,
               [PSK[0], "PAR"], [("MOD", l)])
        for slot in (1, 4):
            ts(MOD[:, l, slot * 8:slot * 8 + 8, :], MOD[:, l, slot * 8:slot * 8 + 8, :], 1.0, 0.0, ALU.add, ALU.add,
               [("MOD", l)], [("MOD", l)])
        for slot in (2, 5):
            ts(MOD[:, l, slot * 8:slot * 8 + 8, :], MOD[:, l, slot * 8:slot * 8 + 8, :], 1.0 / ALPHA, 0.0, ALU.mult, ALU.add,
               [("MOD", l)], [("MOD", l)])

    def phase_mod():
        new_phase()
        stream(list(range(48)), lambda j: wload1(WADA[0, j]), lambda j, h: mod_block(0, j, h), 6)
        mod_finish(0)

    def xk(b):
        return [("XT", b, c) for c in range(8)]

    def norm_tiles(nxt=2, ext=None):
        t = {}
        t["XT"] = [aalloc("XT", [128, 8, 512], F32) for _ in range(nxt)]
        if ext is None:
            t["XB"] = aalloc("XB", [128, 8, 512], BF16)
            t["SQ"] = aalloc("SQ", [128, 8, 512], BF16)
        else:
            S.arena_names.add("XB")
            S.arena_names.add("SQ")
            t["XB"], t["SQ"] = ext
        t["MEAN"] = aalloc("MEAN", [128, 512], F32)
        t["MSQ"] = aalloc("MSQ", [128, 512], F32)
        t["RSTD"] = aalloc("RSTD", [128, 512], F32)
        t["T"] = [aalloc("T", [128, 512], F32) for _ in range(2)]
        t["tk"] = 0
        return t

    import os
    DBG = int(os.environ.get("K_DBG", "99"))

    def stats(t, b, pa, pb, eps=LN_EPS):
        stats_a(t, b)
        stats_b(t, b, pa, pb, eps)

    def stats_a(t, b):
        XT = t["XT"][b]
        cp(t["XB"][:], XT[:], xk(b), ["XB"])
        act(t["SQ"][:], XT[:], AF.Square, xk(b), ["SQ"])

    def stats_b(t, b, pa, pb, eps=LN_EPS):
        XT = t["XT"][b]
        for kc in range(8):
            mm(ps[pa][:], ONESB[:], t["XB"][:, kc, :], kc == 0, kc == 7, ["ONESB", "XB"], [PSK[pa]])
        for kc in range(8):
            mm(ps[pb][:], ONESB[:], t["SQ"][:, kc, :], kc == 0, kc == 7, ["ONESB", "SQ"], [PSK[pb]])
        ts(t["MEAN"][:], ps[pa][:], 1.0 / D, 0.0, ALU.mult, ALU.add, [PSK[pa]], ["MEAN"])
        if DBG <= 2:
            return
        tt(t["MSQ"][:], t["MEAN"][:], t["MEAN"][:], ALU.mult, ["MEAN"], ["MSQ"])
        stt(t["RSTD"][:], ps[pb][:], 1.0 / D, t["MSQ"][:], ALU.mult, ALU.subtract, [PSK[pb], "MSQ"], ["RSTD"])
        act(t["RSTD"][:], t["RSTD"][:], AF.Sqrt, ["RSTD"], ["RSTD"], bias=eps)
        S.add("dve", lambda e: e.reciprocal(out=ps[pb][:], in_=t["RSTD"][:]), ["RSTD"], [PSK[pb]])
        stt(ps[pa][:], t["MEAN"][:], -1.0, ps[pb][:], ALU.mult, ALU.mult, ["MEAN", PSK[pb]], [PSK[pa]])
        t["pab"] = (pa, pb)

    def apply_norm(t, b, outs, okeys, scales, biases, extra_reads):
        XT = t["XT"][b]
        if DBG <= 3:
            return
        for c in range(8):
            k = t["tk"] % 2
            t["tk"] += 1
            T = t["T"][k]
            pa, pb = t["pab"]
            tt(T[:], XT[:, c, :], ps[pb][:], ALU.mult, [("XT", b, c), PSK[pb]], [("T", k)])
            tt(T[:], T[:], ps[pa][:], ALU.add, [("T", k), PSK[pa]], [("T", k)])
            if DBG <= 4:
                continue
            act(outs[c], T[:], AF.Identity, [("T", k)] + extra_reads, [okeys[c]], scale=scales[c], bias=biases[c])

    def hkeys(j):
        return [("H", c, j) for c in range(8)]

    def mod_h(t, b, j, l, s_scale, s_shift):
        g = grp_of(j)
        stats(t, b, 0, 1)
        apply_norm(t, b, [H[:, c, j * 512:(j + 1) * 512] for c in range(8)], hkeys(j),
                   [modap(l, s_scale, c, g) for c in range(8)], [modap(l, s_shift, c, g) for c in range(8)],
                   [("MOD", l)])

    def phase_L0():
        new_phase()
        t = norm_tiles()
        for j in range(NJ):
            b = j % 2
            S.dma(t["XT"][b][:], xin[j], writes=xk(b))
            mod_h(t, b, j, 0, 1, 0)

    def phase_R(l, which):
        new_phase()
        last = (which == 2 and l == n_layers - 1)
        if which == 2:
            S.arena_names.add("PT")
            ext = (nc.alloc_sbuf_tensor_at("XBm%d" % l, [128, 8, 512], BF16, offset=M_OFF + 22528),
                   nc.alloc_sbuf_tensor_at("SQm%d" % l, [128, 8, 512], BF16, offset=M_OFF + 22528 + 8192))
            t = norm_tiles(3, ext)
            PT = [nc.alloc_sbuf_tensor_at("PT%d_%d" % (l, which), [128, NC_FF, 512], BF16, offset=M_OFF),
                  aalloc("PT", [128, NC_FF, 512], BF16)]
        else:
            t = norm_tiles(3)
        gslot = 2 if which == 1 else 5
        lnoff = 0 if which == 1 else 16
        ptk = [0]
        ptsel = {}

        def load_tile(j):
            b = j % 3
            XT = t["XT"][b]
            src = xin if (l == 0 and which == 1) else XD
            S.dma(XT[:], src[j], reads=[("XD", j)], writes=xk(b))
            if which == 2:
                i = j % 2
                for q4 in range(2):
                    S.dma(PT[i][:, q4 * 11:(q4 + 1) * 11, :],
                          PD[q4 * 11:(q4 + 1) * 11, :, j * 512:(j + 1) * 512].rearrange("c p n -> p c n"),
                          reads=[("PD", c) for c in range(NC_FF)], writes=[("PT", i, q4)])

        def ln_step(j, step):
            b = j % 3
            g = grp_of(j)
            XT = t["XT"][b]
            if step == 0:
                stats_a(t, b)
            elif step == 1:
                stats_b(t, b, 0, 1, LN_EPS / (ALPHA * ALPHA))
                apply_norm(t, b, [XT[:, c, :] for c in range(8)], xk(b),
                           [parap(l, P_LN, lnoff + c) for c in range(8)], [parap(l, P_LN, lnoff + 8 + c) for c in range(8)],
                           ["PAR"])
                if last:
                    S.dma(yout[j], XT[:], reads=xk(b), writes=[("yout", j)])
                else:
                    S.dma(XD[j], XT[:], reads=xk(b), writes=[("XD", j)])
                    stats_a(t, b)
                if which == 1 and j == 0:
                    tap("x1_%d" % l, XT[:], [128, 8, 512], F32, xk(b))
            elif step == 2 and not last:
                stats_b(t, b, 0, 1)
                if which == 1:
                    sc, sh, ll = 4, 3, l
                else:
                    sc, sh, ll = 1, 0, l + 1
                apply_norm(t, b, [H[:, c, j * 512:(j + 1) * 512] for c in range(8)], hkeys(j),
                           [modap(ll, sc, c, g) for c in range(8)], [modap(ll, sh, c, g) for c in range(8)], [("MOD", ll)])

        RINGF = RING[:].rearrange("p a b -> p (a b)")
        HW = 11 * 128
        hcnt = [0]
        allring = [("ring", s_) for s_ in range(8)]

        def load_half(j, c, hf):
            hb = hcnt[0] % 5
            first = hcnt[0] == 0
            hcnt[0] += 1
            S.dma(RINGF[:, hb * HW:(hb + 1) * HW], WDOWN[l, c][:, hf * HW:(hf + 1) * HW],
                  writes=[("ringh", hb)] + (allring if first else []), q="pool")
            return hb

        def loader(item):
            j, c = item
            if which == 1:
                return wload1(WO[l, c])
            return [load_half(j, c, 0), load_half(j, c, 1)]

        def compute(item, h):
            j, c = item
            b = j % 3
            g = grp_of(j)
            XT = t["XT"][b]
            if which == 1:
                s_, keys = h
            if c == 0:
                if j + 1 < NJ:
                    pass
                if j > 0:
                    ln_step(j - 1, 0)
            pk = 2 + c % 4
            if which == 1:
                for kc in range(8):
                    mm(ps[pk][:], RING[:, s_, kc * 128:(kc + 1) * 128], M[:, kc, j * 512:(j + 1) * 512], kc == 0, kc == 7,
                       keys + [("M", kc, j)], [PSK[pk]])
            else:
                for kc in range(NC_FF):
                    hb = h[kc // 11]
                    o_ = hb * HW + (kc % 11) * 128
                    mm(ps[pk][:], RINGF[:, o_:o_ + 128], PT[j % 2][:, kc, :],
                       kc == 0, kc == NC_FF - 1, [("ringh", hb), ("PT", j % 2, kc // 11)], [PSK[pk]])
            stt(XT[:, c, :], ps[pk][:], modap(l, gslot, c, g), XT[:, c, :], ALU.mult, ALU.add,
                [PSK[pk], ("XT", b, c), ("MOD", l)], [("XT", b, c)])
            if j > 0 and c == 1:
                ln_step(j - 1, 1)
            if j > 0 and c == 6:
                ln_step(j - 1, 2)
            if c == 7 and j + 1 < NJ:
                pass

        load_tile(0)
        items = []
        for j in range(NJ):
            for c in range(8):
                items.append((j, c))
        hs = {}
        depth = 4 if which == 1 else 2
        n = len(items)
        for i in range(min(depth, n)):
            hs[i] = loader(items[i])
        for i in range(n):
            j, c = items[i]
            if c == 0 and j + 1 < NJ:
                load_tile(j + 1)
            compute(items[i], hs.pop(i))
            if i + depth < n:
                hs[i + depth] = loader(items[i + depth])
        for step in range(3):
            ln_step(NJ - 1, step)
        if which == 2:
            S.add("dve", lambda e: e.memset(DUMMY[:, 1:2], 0.0), [], [("ringh", hb_) for hb_ in range(5)] + allring)

    def merge_phase(l, br, wsrc, nk, rhs_of, rhs_keys, first):
        G = [aalloc("G", [128, 512], BF16) for _ in range(4)]
        TMP = [aalloc("TMPM", [128, 512], BF16) for _ in range(2)]
        cnt = [0]

        def loader(c):
            a = wload1(wsrc[l, c], nelem=nk * 128)
            g_ = wload1(WIN[l, B_GM + br * 8 + c])
            return a, g_

        def compute(c, h):
            (sy, ky), (sg, kg) = h
            for j in range(NJ):
                i = cnt[0] % 4
                cnt[0] += 1
                py, pg = (cnt[0] % 4) * 2, (cnt[0] % 4) * 2 + 1
                for k in range(nk):
                    mm(ps[py][:], RING[:, sy, k * 128:(k + 1) * 128], rhs_of(k, j), k == 0, k == nk - 1,
                       ky + rhs_keys(k, j), [PSK[py]])
                for kc in range(8):
                    mm(ps[pg][:], RING[:, sg, kc * 128:(kc + 1) * 128], H[:, kc, j * 512:(j + 1) * 512], kc == 0, kc == 7,
                       kg + [("H", kc, j)], [PSK[pg]])
                act(G[i][:], ps[pg][:], AF.Sigmoid, [PSK[pg], "PAR"], [("G", i)], bias=parap(l, P_BM, br * 8 + c))
                Mv = M[:, c, j * 512:(j + 1) * 512]
                if first:
                    tt(Mv, ps[py][:], G[i][:], ALU.mult, [PSK[py], ("G", i)], [("M", c, j)])
                else:
                    tt(TMP[i % 2][:], ps[py][:], G[i][:], ALU.mult, [PSK[py], ("G", i)], [("TMPM", i % 2)])
                    tt(Mv, Mv, TMP[i % 2][:], ALU.add, [("M", c, j), ("TMPM", i % 2)], [("M", c, j)])
        stream(list(range(8)), loader, compute, 2)

    def phase_A(l):
        new_phase()
        T1 = aalloc("T1", [128, 4, NTOK], BF16)
        ACS = aalloc("ACS", [128, 16, 4, 256], BF16)
        DFB = [aalloc("DFB", [128, 16, 256], BF16) for _ in range(3)]
        ec = [0]

        def compute0(gi, h):
            s, keys = h
            for j in range(NJ):
                pk = 2 + ec[0] % 4
                for kc in range(8):
                    mm(ps[pk][:], RING[:, s, kc * 128:(kc + 1) * 128], H[:, kc, j * 512:(j + 1) * 512], kc == 0, kc == 7,
                       keys + [("H", kc, j)], [PSK[pk]])
                if ec[0] % 2 == 0:
                    act(T1[:, gi, j * 512:(j + 1) * 512], ps[pk][:], AF.Copy, [PSK[pk]], [("T1", gi, j)])
                else:
                    cp(T1[:, gi, j * 512:(j + 1) * 512], ps[pk][:], [PSK[pk]], [("T1", gi, j)])
                ec[0] += 1
        stream(list(range(4)), lambda gi: wload1(WIN[l, B_AF + gi]), compute0, 4)
        dk = [0]
        for (off, T, latent) in SEQS:
            ntc = T // 128
            for tc in range(ntc):
                tg = (off + tc * 128)
                j = tg // 512
                for half in range(2):
                    pk = 2 + (2 * tc + half) % 4
                    for gg in range(2):
                        gi = half * 2 + gg
                        mm(ps[pk][:, gg * 256:(gg + 1) * 256], T1[:, gi, tg:tg + 128], CSC[:], True, True,
                           [("T1", gi, j), "CSC"], [PSK[pk]])
                    dst = ACS[:, tc, half * 2:half * 2 + 2, :]
                    src = ps[pk][:].rearrange("p (g n) -> p g n", g=2)
                    if (tc + half) % 2 == 0:
                        act(dst, src, AF.Copy, [PSK[pk]], [("ACS", tc, half)])
                    else:
                        cp(dst, src, [PSK[pk]], [("ACS", tc, half)])
            scale = 1.0 / math.sqrt(T * 128.0)
            ntile = T // 256
            for i in range(ntile):
                if latent:
                    bufs = []
                    for cs in range(2):
                        bi = dk[0] % 3
                        dk[0] += 1
                        S.dma(DFB[bi][:], DFTL_d[i, cs], writes=[("DFB", bi)])
                        bufs.append((DFB[bi], [("DFB", bi)]))
                else:
                    bufs = [(DFTP[:, cs], ["DFTP"]) for cs in range(2)]
                tok0 = off + i * 256
                j = tok0 // 512
                for cs in range(2):
                    dbuf, dkeys = bufs[cs]
                    for gi in range(4):
                        pk = 4 + gi
                        for tc in range(ntc):
                            mm(ps[pk][:, 0:256], ACS[:, tc, gi, cs * 128:(cs + 1) * 128], dbuf[:, tc, :],
                               cs == 0 and tc == 0, cs == 1 and tc == ntc - 1,
                               [("ACS", tc, gi // 2)] + dkeys, [PSK[pk]])
                for gi in range(4):
                    pk = 4 + gi
                    src = ps[pk][:, 0:256]
                    act(T1[:, gi, tok0:tok0 + 256], src, AF.Copy, [PSK[pk]], [("T1", gi, j)], scale=scale)
        tap("afo_%d" % l, T1[:], [128, 4, NTOK], BF16, [("T1", gi, j) for gi in range(4) for j in range(NJ)])
        merge_phase(l, 0, WFFT, 4, lambda k, j: T1[:, k, j * 512:(j + 1) * 512], lambda k, j: [("T1", k, j)], True)

    def phase_B(l):
        new_phase()
        T1 = aalloc("T1", [128, 4, NTOK], BF16)
        VG = [aalloc("VG", [128, 512], F32) for _ in range(2)]
        VN = [aalloc("VN", [128, 512], BF16) for _ in range(2)]
        BST = [aalloc("BST", [128, 8], F32) for _ in range(2)]
        S.dma(SGUG[:], SGUG_d[l], writes=["SGUG"])
        S.dma(WST[:], WST_d[l], writes=["WST"], q="pool")
        S.dma(SGUB[:], SGUB_d[l], writes=["SGUB"], q="pool")
        ec = [0]

        def compute_u(gi, h):
            s, keys = h
            for j in range(NJ):
                pk = 2 + ec[0] % 4
                ec[0] += 1
                for kc in range(8):
                    mm(ps[pk][:], RING[:, s, kc * 128:(kc + 1) * 128], H[:, kc, j * 512:(j + 1) * 512], kc == 0, kc == 7,
                       keys + [("H", kc, j)], [PSK[pk]])
                act(T1[:, gi, j * 512:(j + 1) * 512], ps[pk][:], AF.Gelu_apprx_tanh, [PSK[pk]], [("T1", gi, j)])
        stream(list(range(4)), lambda gi: wload1(WIN[l, B_U + gi]), compute_u, 4)
        s0, vkeys = ring_load(None, 4, align=4)
        for gi in range(4):
            S.dma(RING[:, s0 + gi, :], WIN[l, B_V + gi], writes=[("ring", s0 + gi)], q="pool")
        def v_mm(tt_):
            j = tt_ // 4
            i = tt_ % 2
            pk = 2 + tt_ % 2
            for kc in range(8):
                mm(ps[pk][:], H[:, kc, tt_ * 128:(tt_ + 1) * 128], RING[:, s0:s0 + 4, kc * 128:(kc + 1) * 128], kc == 0, kc == 7,
                   vkeys + [("H", kc, j)], [PSK[pk]])
            act(VG[i][:], ps[pk][:], AF.Gelu_apprx_tanh, [PSK[pk]], [("VG", i)])

        v_mm(0)
        for tt_ in range(NTT):
            j = tt_ // 4
            i = tt_ % 2
            if tt_ + 1 < NTT:
                v_mm(tt_ + 1)
            S.add("dve", lambda e, i=i: e.bn_stats(out=BST[i][:, 0:6], in_=VG[i][:]), [("VG", i)], [("BST", i)])
            S.add("dve", lambda e, i=i: e.bn_aggr(out=BST[i][:, 6:8], in_=BST[i][:, 0:6]), [("BST", i)], [("BST", i)])
            ts(BST[i][:, 7:8], BST[i][:, 7:8], LN_EPS, 0.0, ALU.add, ALU.add, [("BST", i)], [("BST", i)])
            tt(BST[i][:, 7:8], BST[i][:, 7:8], NEGH[:, 0:1], ALU.pow, [("BST", i), "NEGH"], [("BST", i)], eng="pool")
            ts(VG[i][:], VG[i][:], BST[i][:, 6:7], BST[i][:, 7:8], ALU.subtract, ALU.mult, [("VG", i), ("BST", i)], [("VG", i)])
            tt(VN[i][:], VG[i][:], SGUG[:], ALU.mult, [("VG", i), "SGUG"], [("VN", i)])
            pv = 4 + tt_ % 2
            for gi in range(4):
                mm(ps[pv][:, gi * 128:(gi + 1) * 128], VN[i][:, gi * 128:(gi + 1) * 128], WST[:, gi, :], True, False,
                   [("VN", i), "WST"], [PSK[pv]])
                mm(ps[pv][:, gi * 128:(gi + 1) * 128], ONESB[0:1, :], SGUB[0:1, gi * 128:(gi + 1) * 128], False, True,
                   ["ONESB", "SGUB"], [PSK[pv]])
            uv = T1[:, :, tt_ * 128:(tt_ + 1) * 128]
            tt(uv, uv, ps[pv][:].rearrange("p (g n) -> p g n", g=4), ALU.mult,
               [PSK[pv]] + [("T1", gi, j) for gi in range(4)], [("T1", gi, j) for gi in range(4)])
        tap("uv_%d" % l, T1[:], [128, 4, NTOK], BF16, [("T1", gi, j) for gi in range(4) for j in range(NJ)])
        merge_phase(l, 1, WSGU, 4, lambda k, j: T1[:, k, j * 512:(j + 1) * 512], lambda k, j: [("T1", k, j)], False)

    def phase_C(l):
        new_phase()
        QK = aalloc("QK", [128, 8, NTOK], BF16)
        SF = aalloc("SF", [128, 2, 4, 256], F32)
        SB = aalloc("SB", [128, 2, 4, 256], BF16)
        ZARENA = acur[0]
        LRT = aalloc("LRT", [32, NTOK], BF16)
        S.dma(W2P[:], W2P_d[l], writes=["W2P"], q="pool")
        S.dma(GLB[:], GLB_d[l], writes=["GLB"], q="pool")
        S.dma(GNG[:], GNG_d[l], writes=["GNG"])
        STG = [aalloc("STG", [128, 512], BF16) for _ in range(3)]
        EZ = [aalloc("EZ", [128, 512], F32) for _ in range(2)]
        ec = [0]

        def compute_qk(bi, h):
            s, keys = h
            for j in range(NJ):
                pk = 2 + ec[0] % 4
                ec[0] += 1
                for kc in range(8):
                    mm(ps[pk][:], RING[:, s, kc * 128:(kc + 1) * 128], H[:, kc, j * 512:(j + 1) * 512], kc == 0, kc == 7,
                       keys + [("H", kc, j)], [PSK[pk]])
                dst = QK[:, bi, j * 512:(j + 1) * 512]
                if bi < 4:
                    act(dst, ps[pk][:], AF.Copy, [PSK[pk]], [("QK", bi, j)], scale=128.0 ** -0.5)
                else:
                    cp(dst, ps[pk][:], [PSK[pk]], [("QK", bi, j)])
        stream(list(range(8)), lambda bi: wload1(WIN[l, B_Q + bi]), compute_qk, 4)
        s, keys = wload1(WIN[l, B_LR])
        for j in range(NJ):
            pk = 2 + j % 4
            for kc in range(8):
                mm(ps[pk][0:32, :], RING[:, s, kc * 128:kc * 128 + 32], H[:, kc, j * 512:(j + 1) * 512], kc == 0, kc == 7,
                   keys + [("H", kc, j)], [PSK[pk]])
            cp(LRT[:, j * 512:(j + 1) * 512], ps[pk][0:32, :], [PSK[pk]], [("LRT", j)])
        sk = [0]
        for grp in range(5):
            blk0 = B_K + grp * 4
            s0, gkeys = ring_load(None, 4, align=4)
            for gi in range(4):
                S.dma(RING[:, s0 + gi, :], WIN[l, blk0 + gi], writes=[("ring", s0 + gi)], q="pool")
            for tt_ in range(NTT):
                j = tt_ // 4
                pk = 2 + tt_ % 4
                for kc in range(8):
                    mm(ps[pk][:], H[:, kc, tt_ * 128:(tt_ + 1) * 128], RING[:, s0:s0 + 4, kc * 128:(kc + 1) * 128],
                       kc == 0, kc == 7, gkeys + [("H", kc, j)], [PSK[pk]])
                i = sk[0] % 3
                sk[0] += 1
                if grp >= 3:
                    act(STG[i][:], ps[pk][:], AF.Silu, [PSK[pk]], [("STG", i)])
                elif tt_ % 2 == 0:
                    act(STG[i][:], ps[pk][:], AF.Copy, [PSK[pk]], [("STG", i)])
                else:
                    cp(STG[i][:], ps[pk][:], [PSK[pk]], [("STG", i)])
                S.dma(KVR[tt_, :, grp * 512:(grp + 1) * 512], STG[i][:], reads=[("STG", i)], writes=[("KVR", tt_, grp)])
        for tt_ in range(NTT):
            j = tt_ // 4
            for d in range(2):
                pk = 2 + (2 * tt_ + d) % 4
                mm(ps[pk][:], LRT[:, tt_ * 128:(tt_ + 1) * 128], W2P[:, d, :], True, False, [("LRT", j), "W2P"], [PSK[pk]])
                mm(ps[pk][:], ONESB[0:1, :], GLB[0:1, d * 512:(d + 1) * 512], False, True, ["ONESB", "GLB"], [PSK[pk]])
                e = (2 * tt_ + d) % 2
                act(EZ[e][:], ps[pk][:], AF.Exp, [PSK[pk]], [("EZ", e)], scale=-1.0)
                i = sk[0] % 3
                sk[0] += 1
                act(STG[i][:], EZ[e][:], AF.Ln, [("EZ", e)], [("STG", i)], bias=1.0)
                S.dma(SPD[tt_, :, d, :], STG[i][:], reads=[("STG", i)], writes=[("SPD", tt_, d)])

        S.fence(DUMMY[:, 0:1])
        acur[0] = ZARENA
        W = {}
        for d in range(2):
            W[d] = dict(
                KV=[aalloc("KV", [128, 1536], BF16) for _ in range(2)], SPT=[aalloc("SPT", [128, 512], BF16) for _ in range(2)],
                KD=aalloc("KD", [128, 512], BF16), EKD=aalloc("EKD", [128, 512], BF16),
                EG=aalloc("EG", [128, 4, 128], F32), ENG=aalloc("ENG", [128, 4, 128], BF16),
                QG=aalloc("QG", [128, 4, 128], BF16), KG=aalloc("KG", [128, 4, 128], BF16),
                ATM=aalloc("ATM", [128, 4, 128], BF16), OS=[aalloc("OS", [128, 256], F32) for _ in range(2)])

        def load_kv(d, tt_, p):
            w = W[d]
            S.dma(w["KV"][p][:], KVR[tt_, :, 0:1536], reads=[("KVR", tt_, g_) for g_ in range(3)], writes=[("KV", d, p)])
            S.dma(w["SPT"][p][:], SPD[tt_, :, d, :], reads=[("SPD", tt_, d)], writes=[("SPT", d, p)])

        def stage1(d, tt_, p):
            w = W[d]
            j = tt_ // 4
            tok = tt_ * 128
            kv, spt = ("KV", d, p), ("SPT", d, p)
            SPTt, KVt = w["SPT"][p], w["KV"][p]
            pD, pG = 0, 1
            mm(ps[pD][:], TMS[:, d, 1, :], SPTt[:], True, True, ["TMS", spt], [PSK[pD]])
            for h in range(4):
                mm(ps[pG][:, h * 128:(h + 1) * 128], SPTt[:, h * 128:(h + 1) * 128], TMS[:, d, 0, :], True, True,
                   ["TMS", spt], [PSK[pG]])
            gv = ps[pG][:].rearrange("p (h n) -> p h n", h=4)
            act(w["EG"][:], gv, AF.Exp, [PSK[pG]], [("EG", d)])
            act(w["ENG"][:], gv, AF.Exp, [PSK[pG]], [("ENG", d)], scale=-1.0)
            act(w["EKD"][:], ps[pD][:], AF.Exp, [PSK[pD]], [("EKD", d)])
            tt(w["QG"][:], QK[:, 0:4, tok:tok + 128], w["EG"][:], ALU.mult, [("QK", b_, j) for b_ in range(4)] + [("EG", d)],
               [("QG", d)])
            tt(w["KG"][:], QK[:, 4:8, tok:tok + 128], w["ENG"][:], ALU.mult, [("QK", b_, j) for b_ in range(4, 8)] + [("ENG", d)],
               [("KG", d)])
            tt(w["KD"][:], KVt[:, 0:512], w["EKD"][:], ALU.mult, [kv, ("EKD", d)], [("KD", d)])

        def stage2(d, tt_, p):
            w = W[d]
            pA = 0
            for h in range(4):
                mm(ps[pA][:, h * 128:(h + 1) * 128], w["KG"][:, h, :], w["QG"][:, h, :], True, True,
                   [("KG", d), ("QG", d)], [PSK[pA]])
            tt(w["ATM"][:], ps[pA][:].rearrange("p (h n) -> p h n", h=4), MASKA[:, d], ALU.mult, [PSK[pA], "MASKA"], [("ATM", d)])

        def chunk_step(d, h, c, p):
            w = W[d]
            kv = ("KV", d, p)
            KVt = w["KV"][p]
            pO = 3 + h
            r0, r1 = c * 64, (c + 1) * 64
            mm(ps[pO][r0:r1, 0:256], w["QG"][:, h, r0:r1], SB[:, d, h, :], False, True,
               [("QG", d), ("SB", d, h)], [PSK[pO]])
            pS = 2 if h % 2 == 0 else 7
            mm(ps[pS][:, 0:256], w["KD"][r0:r1, h * 128:(h + 1) * 128],
               KVt[r0:r1, 512 + h * 256:512 + (h + 1) * 256], True, True, [("KD", d), kv], [PSK[pS]])
            lend = (c * 64 + 63) if d == 0 else c * 64
            stt(SF[:, d, h, :], SF[:, d, h, :], w["EG"][:, h, lend:lend + 1], ps[pS][:, 0:256], ALU.mult, ALU.add,
                [("SF", d, h), ("EG", d), PSK[pS]], [("SF", d, h)])
            act(SB[:, d, h, :], SF[:, d, h, :], AF.Copy, [("SF", d, h)], [("SB", d, h)])

        def part1(d, tt_, p):
            w = W[d]
            corder = (0, 1) if d == 0 else (1, 0)
            for h in range(4):
                mm(ps[3 + h][:, 0:256], w["ATM"][:, h, :], w["KV"][p][:, 512 + h * 256:512 + (h + 1) * 256], True, False,
                   [("ATM", d), ("KV", d, p)], [PSK[3 + h]])
                chunk_step(d, h, corder[0], p)

        def part2(d, tt_, p):
            w = W[d]
            corder = (0, 1) if d == 0 else (1, 0)
            for h in range(4):
                chunk_step(d, h, corder[1], p)
                o_ = w["OS"][h % 2]
                if h % 2 == 0:
                    act(o_[:], ps[3 + h][:, 0:256], AF.Copy, [PSK[3 + h]], [("OS", d, h % 2)])
                else:
                    cp(o_[:], ps[3 + h][:, 0:256], [PSK[3 + h]], [("OS", d, h % 2)])
                S.dma(OD[d, tt_, :, h * 256:(h + 1) * 256], o_[:], reads=[("OS", d, h % 2)], writes=[("OD", d, tt_, h)])

        for si, (off, T, latent) in enumerate(SEQS):
            nt = T // 128
            t0 = off // 128
            for d in range(2):
                for h in range(4):
                    if latent:
                        S.dma(SF[:, d, h, :], st0[l, d, h], writes=[("SF", d, h)])
                    else:
                        S.add("dve", lambda e, d=d, h=h: e.memset(SF[:, d, h, :], 0.0), [], [("SF", d, h)])
                    act(SB[:, d, h, :], SF[:, d, h, :], AF.Copy, [("SF", d, h)], [("SB", d, h)])
            seq = []
            for i in range(nt):
                seq.append((0, t0 + i, i % 2))
                seq.append((1, t0 + nt - 1 - i, i % 2))
            n_ = len(seq)
            load_kv(*seq[0])
            load_kv(*seq[1])
            stage1(*seq[0])
            stage2(*seq[0])
            for i, (d, t_, p) in enumerate(seq):
                if i + 2 < n_:
                    load_kv(*seq[i + 2])
                if i + 1 < n_:
                    stage1(*seq[i + 1])
                part1(d, t_, p)
                if i + 1 < n_:
                    stage2(*seq[i + 1])
                part2(d, t_, p)
            if not latent:
                for d in range(2):
                    for h in range(4):
                        S.dma(nsout[si - 1, l, d, h], SF[:, d, h, :], reads=[("SF", d, h)], writes=[("ns", si, l, d, h)])

        new_phase()
        OGT = aalloc("OGT", [128, 8, NTOK], BF16)
        OF = [aalloc("OF", [128, 1024], F32) for _ in range(3)]
        OB = [aalloc("OB", [128, 1024], F32) for _ in range(3)]
        RT = [aalloc("RT", [128, 1024], BF16) for _ in range(3)]
        RG = [aalloc("RG", [128, 1024], BF16) for _ in range(3)]
        OG = [aalloc("OG", [128, 1024], BF16) for _ in range(3)]
        SQJ = aalloc("SQJ", [128, 256], BF16)
        SS = [aalloc("SS", [128, 8], F32) for _ in range(3)]
        for tt_ in range(NTT):
            i = tt_ % 3
            j = tt_ // 4
            S.dma(OF[i][:], OD[0, tt_], reads=[("OD", 0, tt_, h_) for h_ in range(4)], writes=[("OF", i)])
            S.dma(OB[i][:], OD[1, tt_], reads=[("OD", 1, tt_, h_) for h_ in range(4)], writes=[("OB", i)])
            S.dma(RT[i][:], KVR[tt_, :, 1536:2560], reads=[("KVR", tt_, 3), ("KVR", tt_, 4)], writes=[("RT", i)])
            tt(OF[i][:], OF[i][:], OB[i][:], ALU.add, [("OF", i), ("OB", i)], [("OF", i)])
            for h in range(4):
                act(SQJ[:], OF[i][:, h * 256:(h + 1) * 256], AF.Square, [("OF", i)], ["SQJ", ("SS", i)],
                    accum_out=SS[i][:, h:h + 1])
            act(SS[i][:, 4:8], SS[i][:, 0:4], AF.Sqrt, [("SS", i)], [("SS", i)], bias=LN_EPS, scale=1.0 / 256)
            S.add("dve", lambda e, i=i: e.reciprocal(out=SS[i][:, 4:8], in_=SS[i][:, 4:8]), [("SS", i)], [("SS", i)])
            for h in range(4):
                tt(RG[i][:, h * 256:(h + 1) * 256], RT[i][:, h * 256:(h + 1) * 256], GNG[:], ALU.mult, [("RT", i), "GNG"],
                   [("RG", i)], eng="pool")
            for h in range(4):
                stt(OG[i][:, h * 256:(h + 1) * 256], OF[i][:, h * 256:(h + 1) * 256], SS[i][:, 4 + h:5 + h],
                    RG[i][:, h * 256:(h + 1) * 256], ALU.mult, ALU.mult, [("OF", i), ("SS", i), ("RG", i)], [("OG", i)])
            pk = 2 + tt_ % 2
            pb = ps[pk][:].bitcast(BF16)
            for c in range(8):
                S.add("pe", lambda e, c=c, pb=pb, i=i: e.transpose(pb[:, c * 128:(c + 1) * 128], OG[i][:, c * 128:(c + 1) * 128], IDB[:]),
                      [("OG", i), "IDB"], [PSK[pk]])
            dst = OGT[:, :, tt_ * 128:(tt_ + 1) * 128]
            src = pb[:, 0:1024].rearrange("p (c n) -> p c n", c=8)
            if tt_ % 2 == 0:
                act(dst, src, AF.Copy, [PSK[pk]], [("OGT", tt_)])
            else:
                cp(dst, src, [PSK[pk]], [("OGT", tt_)])
        tap("ogt_%d" % l, OGT[:], [128, 8, NTOK], BF16, [("OGT", t_) for t_ in range(NTT)])
        merge_phase(l, 2, WGLA, 8, lambda k, j: OGT[:, k, j * 512:(j + 1) * 512],
                    lambda k, j: [("OGT", 4 * j + q_) for q_ in range(4)], False)
        tap("m_%d" % l, M[:], [128, 8, NTOK], BF16, [("M", c, j) for c in range(8) for j in range(NJ)])

    PADL = 65
    SEGOFF = [PADL, PADL + 2048 + 66, PADL + 2048 + 66 + 256 + 2]
    AW = SEGOFF[2] + 256 + 2

    def phase_FU(l):
        new_phase()
        AV = [aalloc("AV", [128, 3, AW], BF16) for _ in range(2)]
        DG = [aalloc("DG", [128, 9, 128], BF16) for _ in range(2)]
        CG = [aalloc("CG", [128, 512], BF16) for _ in range(2)]
        PC = [aalloc("PC", [128, NTOK], BF16) for _ in range(2)]
        CM = aalloc("CM", [128, 2, 2048], BF16)
        S.add("pool", lambda e: e.memset(CM[:], 1.0), [], ["CM"])
        cmv = CM[:].rearrange("p a (r c) -> p a r c", c=64)
        S.add("pool", lambda e: e.memset(cmv[:, 0, :, 63:64], 0.0), ["CM"], ["CM"])
        S.add("pool", lambda e: e.memset(cmv[:, 1, :, 0:1], 0.0), ["CM"], ["CM"])
        for a in range(2):
            S.add("pool", lambda e, a=a: e.memset(AV[a][:], 0.0), [], [("AV", a)])
        cg = [0]

        def loader(c):
            return wload1(WUP[l, c]), wload1(WUP[l, NC_FF + c])

        pend = []

        def compute(c, h):
            (sa, ka), (sg, kg) = h
            a = c % 2
            A = AV[a]
            for tap_i in range(9):
                ts(DG[a][:, tap_i, :], IDB[:], parap(l, P_WDW, tap_i * NC_FF + c), 0.0, ALU.mult, ALU.add, ["IDB", "PAR"],
                   [("DG", a)], eng="pool")
            for j in range(NJ):
                pk = 2 + j % 2
                for kc in range(8):
                    mm(ps[pk][:], RING[:, sa, kc * 128:(kc + 1) * 128], H[:, kc, j * 512:(j + 1) * 512], kc == 0, kc == 7,
                       ka + [("H", kc, j)], [PSK[pk]])
                if j < 4:
                    o = SEGOFF[0] + j * 512
                    act(A[:, 1, o:o + 512], ps[pk][:], AF.Copy, [PSK[pk]], [("AV", a)])
                    tt(A[:, 0, o:o + 512], ps[pk][:], CM[:, 0, j * 512:(j + 1) * 512], ALU.mult, [PSK[pk], "CM"], [("AV", a)])
                    tt(A[:, 2, o:o + 512], ps[pk][:], CM[:, 1, j * 512:(j + 1) * 512], ALU.mult, [PSK[pk], "CM"], [("AV", a)])
                else:
                    for s_ in range(2):
                        o = SEGOFF[1 + s_]
                        act(A[:, 1, o:o + 256], ps[pk][:, s_ * 256:(s_ + 1) * 256], AF.Copy, [PSK[pk]], [("AV", a)])
            if l + 1 < n_layers:
                for (jb, hb) in pend:
                    mod_block(l + 1, jb, hb)
            for j in range(NJ):
                pg = 4 + j % 2
                pc = 6 + j % 2
                for kc in range(8):
                    mm(ps[pg][:], RING[:, sg, kc * 128:(kc + 1) * 128], H[:, kc, j * 512:(j + 1) * 512], kc == 0, kc == 7,
                       kg + [("H", kc, j)], [PSK[pg]])
                if j < 4:
                    o = SEGOFF[0] + j * 512
                    for tap_i in range(9):
                        dy, dx = tap_i // 3, tap_i % 3
                        sh = (dy - 1) * 64 + (dx - 1)
                        mm(ps[pc][:], DG[a][:, tap_i, :], A[:, dx, o + sh:o + sh + 512], tap_i == 0, tap_i == 8,
                           [("DG", a), ("AV", a)], [PSK[pc]])
                else:
                    for s_ in range(2):
                        o = SEGOFF[1 + s_]
                        for dx in range(3):
                            mm(ps[pc][:, s_ * 256:(s_ + 1) * 256], DG[a][:, 3 + dx, :], A[:, 1, o + dx - 1:o + dx - 1 + 256],
                               dx == 0, dx == 2, [("DG", a), ("AV", a)], [PSK[pc]])
                i = cg[0] % 2
                cg[0] += 1
                act(CG[i][:], ps[pc][:], AF.Gelu_apprx_tanh, [PSK[pc], "PAR"], [("CG", i)], bias=parap(l, P_BDW, c))
                tt(PC[a][:, j * 512:(j + 1) * 512], ps[pg][:], CG[i][:], ALU.mult, [PSK[pg], ("CG", i)], [("PC", a, j)])
            S.dma(PD[c], PC[a][:], reads=[("PC", a, j) for j in range(NJ)], writes=[("PD", c)])
            if l + 1 < n_layers:
                pend[:] = [(jb, wload1(WADA[l + 1, jb])) for jb in range(3 * c, min(48, 3 * c + 3))]
        stream(list(range(NC_FF)), loader, compute, 2)
        if l + 1 < n_layers:
            for (jb, hb) in pend:
                mod_block(l + 1, jb, hb)
            mod_finish(l + 1)

    phase_mod()
    if stop_after != "MOD":
        phase_L0()
    if DBG > 5:
        tap("h0", H[:], [128, 8, NTOK], BF16, [("H", c, j) for c in range(8) for j in range(NJ)])
    tap("mod", MOD[:, 0:n_layers], [128, n_layers, 48, 2], F32, [("MOD", l_) for l_ in range(n_layers)])
    done = stop_after in ("L0", "MOD")
    for l in range(n_layers):
        if done:
            break
        for name, fn in (("A", phase_A), ("B", phase_B), ("C", phase_C), ("R1", lambda l: phase_R(l, 1)),
                         ("FU", phase_FU), ("R2", lambda l: phase_R(l, 2))):
            fn(l)
            if stop_after == "%s%d" % (name, l):
                done = True
                break
        tap("h_%d" % (l + 1), H[:], [128, 8, NTOK], BF16, [("H", c, j) for c in range(8) for j in range(NJ)])
    tap("modend", MOD[:, 0:n_layers], [128, n_layers, 48, 2], F32, [("MOD", l_) for l_ in range(n_layers)])
    new_phase()
    S.emit()
    return nc, tap_out


def _tile_w(w):
    K, N = w.shape
    return np.ascontiguousarray(w.reshape(K // 128, 128, N // 128, 128).transpose(2, 1, 0, 3).reshape(N // 128, 128, K))


def _fm(v):
    n = v.shape[-1] // 128
    return np.moveaxis(v.reshape(v.shape[:-1] + (n, 128)), -1, 0)


def _consts():
    bf = ml_dtypes.bfloat16
    c = {}
    c["IDB"] = np.eye(128, dtype=np.float32).astype(bf)
    k = np.arange(128)
    ang = 2 * np.pi * np.outer(k, k) / 128.0
    c["CSC"] = np.concatenate([np.cos(ang), np.sin(ang)], axis=1).astype(np.float32).astype(bf)
    t = np.arange(256)
    a = 2 * np.pi * ((np.outer(t, t)) % 256) / 256.0
    dp = np.stack([np.cos(a), -np.sin(a)], 0).reshape(2, 2, 128, 256)
    c["DFTP"] = np.ascontiguousarray(dp.transpose(2, 0, 1, 3)).astype(np.float32).astype(bf)
    t = np.arange(2048, dtype=np.int64)
    a = 2 * np.pi * ((np.outer(t, t)) % 2048) / 2048.0
    dl = np.stack([np.cos(a), -np.sin(a)], 0).reshape(2, 16, 128, 8, 256)
    c["DFTL"] = np.ascontiguousarray(dl.transpose(3, 0, 2, 1, 4)).astype(np.float32).astype(bf)
    m = np.arange(128)[:, None]
    l = np.arange(128)[None, :]
    same = (m // 64) == (l // 64)
    tms = np.zeros((128, 2, 2, 128), np.float32)
    tms[:, 0, 0] = np.where(same & (m <= l), -1.0 / 16, 0)
    tms[:, 0, 1] = np.where(same & (m > l), -1.0 / 16, 0)
    tms[:, 1, 0] = np.where(same & (m >= l), -1.0 / 16, 0)
    tms[:, 1, 1] = np.where(same & (m < l), -1.0 / 16, 0)
    c["TMS"] = tms.astype(bf)
    ma = np.zeros((128, 2, 4, 128), np.float32)
    ma[:, 0] = np.where(same & (m <= l), 1.0, 0)[:, None, :]
    ma[:, 1] = np.where(same & (m >= l), 1.0, 0)[:, None, :]
    c["MASKA"] = ma.astype(bf)
    return c


def prep_shared(w_in, b_merge, w_fft_out, sgu_g, sgu_ws, sgu_b, w_sgu_out, gla_w2, gla_b, gla_norm_g, w_gla_out, w_o,
                ln1_g, ln1_b, w_ada, b_ada, w_up, w_dw, b_dw, w_down, ln2_g, ln2_b):
    L = DEPTH
    f = lambda a: np.asarray(a, dtype=np.float32)
    w_in, w_ada, w_up, w_down = f(w_in), f(w_ada), f(w_up), f(w_down)
    sh = {}
    wext = np.concatenate([w_in[:, :, :4640], np.zeros((L, D, 96), np.float32), w_in[:, :, 4640:]], axis=2)
    sh["WIN"] = np.stack([_tile_w(wext[l]) for l in range(L)])
    sh["WFFT"] = np.stack([_tile_w(f(w_fft_out)[l]) for l in range(L)])
    sh["WSGU"] = np.stack([_tile_w(f(w_sgu_out)[l]) for l in range(L)])
    sh["WGLA"] = np.stack([_tile_w(f(w_gla_out)[l]) for l in range(L)])
    sh["WO"] = np.stack([_tile_w(f(w_o)[l]) for l in range(L)])
    sh["WUP"] = np.stack([_tile_w(w_up[l]) for l in range(L)])
    sh["WDOWN"] = np.stack([_tile_w(w_down[l]) for l in range(L)])
    sh["WADA"] = np.stack([_tile_w(w_ada[l]) for l in range(L)])
    par = np.zeros((128, L, PL), np.float32)
    par[:, :, P_BADA:P_BADA + 48] = _fm(f(b_ada))
    par[:, :, P_BM:P_BM + 24] = _fm(f(b_merge)).reshape(128, L, 24)
    lnp = np.stack([f(ln1_g), f(ln1_b), f(ln2_g), f(ln2_b)], axis=1)
    par[:, :, P_LN:P_LN + 32] = _fm(lnp).reshape(128, L, 32)
    par[:, :, P_BDW:P_BDW + 22] = _fm(f(b_dw))
    par[:, :, P_WDW:P_WDW + 198] = _fm(f(w_dw).reshape(L, 9, D_FF)).reshape(128, L, 198)
    sh["PAR"] = np.ascontiguousarray(par.reshape(128, L * PL))
    sh["SGUG"] = np.ascontiguousarray(np.broadcast_to(f(sgu_g)[:, None, :], (L, 128, 512)))
    sh["GNG"] = np.ascontiguousarray(np.broadcast_to(f(gla_norm_g)[:, None, :], (L, 128, 256)))
    sh["WST"] = np.ascontiguousarray(f(sgu_ws).transpose(0, 3, 1, 2))
    w2p = np.zeros((L, 32, 2, 512), np.float32)
    for d in range(2):
        w2p[:, d * 16:(d + 1) * 16, d, :] = f(gla_w2)[:, d]
    sh["W2P"] = w2p
    sh["GLB"] = np.ascontiguousarray(f(gla_b).reshape(L, 1, 1024))
    sh["SGUB"] = np.ascontiguousarray(f(sgu_b).reshape(L, 1, 512))
    sh.update(_consts())
    return sh


def _to_fm_tiles(x_tok):
    return np.ascontiguousarray(x_tok.reshape(NJ, 512, 8, 128).transpose(0, 3, 2, 1))


def _from_fm_tiles(y):
    return y.transpose(0, 3, 2, 1).reshape(NTOK, D)


def prep_core(i, x_prompt, x_sample, state_gla, c, c_ctx):
    xt = np.concatenate([x_sample[i], x_prompt[2 * i], x_prompt[2 * i + 1]], axis=0).astype(np.float32)
    m = {"xin": _to_fm_tiles(xt)}
    cc = np.stack([np.asarray(c[i], np.float32), np.asarray(c_ctx, np.float32)], axis=-1)
    m["cT"] = np.ascontiguousarray(cc.reshape(8, 128, 2).transpose(1, 0, 2))
    m["st0"] = np.ascontiguousarray(np.asarray(state_gla[i], np.float32))
    return m


_CACHE = {}


def kernel(x_prompt, x_sample, state_gla, c, c_ctx, **w):
    x_prompt, x_sample = np.asarray(x_prompt), np.asarray(x_sample)
    state_gla, c, c_ctx = np.asarray(state_gla), np.asarray(c), np.asarray(c_ctx)
    sh = prep_shared(**{k: np.asarray(v) for k, v in w.items()})
    if "nc" not in _CACHE:
        _CACHE["nc"] = build()[0]
    nc = _CACHE["nc"]
    in_maps = []
    for i in range(8):
        m = dict(sh)
        m.update(prep_core(i, x_prompt, x_sample, state_gla, c, c_ctx))
        in_maps.append(m)
    res = run_bass_kernel_spmd(nc, in_maps, core_ids=list(range(8)))
    y_prompt = np.zeros((16, 256, D), np.float32)
    y_sample = np.zeros((8, 2048, D), np.float32)
    ns = np.zeros((16, DEPTH, 2, 4, 128, 256), np.float32)
    for i in range(8):
        r = res.results[i]
        y = _from_fm_tiles(np.asarray(r["yout"], np.float32))
        y_sample[i] = y[:2048]
        y_prompt[2 * i] = y[2048:2304]
        y_prompt[2 * i + 1] = y[2304:2560]
        nso = np.asarray(r["nsout"], np.float32)
        ns[2 * i] = nso[0]
        ns[2 * i + 1] = nso[1]
    return (y_prompt, y_sample, ns)
```
